# Optimizing a Trainium2 kernel written in Bass

```python
import jax, jax.numpy as jnp
from jax import lax
import numpy as np

D_MODEL = 2048
BATCH = 8
SEQ = 2048
DEPTH = 2

D_MIX = D_MODEL
D_LRU = D_MIX // 2
D_RWKV = D_MIX - D_LRU
LRU_HEADS = 4
LRU_BLOCK = D_LRU // LRU_HEADS
LRU_CONV = 4
LRU_C = 8.0
RWKV_HEAD = 64
RWKV_HEADS = D_RWKV // RWKV_HEAD
LORA_W = 64
LORA_A = 64
LORA_V = 32
LORA_G = 160
N_SHIFT = 3 * D_RWKV + LORA_W + LORA_A + LORA_G
D_IN = 2 * D_LRU + N_SHIFT
D_FF = 3 * D_MODEL
FFN_CONV = 3
D_PLE = 256
RMS_EPS = 1e-6
LNX_EPS = 64e-5

kernel_name = 'hybrid_rglru_rwkv7_parallel_heads'


def rmsnorm(x, g):
    xf = x.astype(jnp.float32)
    y = xf * lax.rsqrt(jnp.mean(xf * xf, axis=-1, keepdims=True) + RMS_EPS)
    return y.astype(x.dtype) * g


def causal_dwconv(x, w, b):
    K = w.shape[0]
    S = x.shape[1]
    xp = jnp.pad(x, ((0, 0), (K - 1, 0), (0, 0)))
    out = xp[:, K - 1:K - 1 + S] * w[K - 1] + b
    for j in range(K - 1):
        out = out + xp[:, j:j + S] * w[j]
    return out


def token_shift(z):
    return jnp.pad(z, ((0, 0), (1, 0), (0, 0)))[:, :-1]


def _linear_combine(left, right):
    a_l, b_l = left
    a_r, b_r = right
    return a_l * a_r, a_r * b_l + b_r


def rg_lru(xc, wx, bx, wa, ba, lam):
    B, S, _ = xc.shape
    xh = xc.reshape(B, S, LRU_HEADS, LRU_BLOCK)
    gate_x = jax.nn.sigmoid(jnp.einsum('bshi,hij->bshj', xh, wx).reshape(B, S, D_LRU) + bx)
    gate_a = jax.nn.sigmoid(jnp.einsum('bshi,hij->bshj', xh, wa).reshape(B, S, D_LRU) + ba)
    log_a = (-LRU_C * gate_a * jax.nn.softplus(-lam)).astype(jnp.float32)
    a = jnp.exp(log_a)
    mult = jnp.sqrt(1.0 - jnp.exp(2.0 * log_a))
    mult = jnp.where((jnp.arange(S) == 0)[None, :, None], 1.0, mult)
    b_in = (xc * gate_x).astype(jnp.float32) * mult
    _, h = lax.associative_scan(_linear_combine, (a, b_in), axis=1)
    return h.astype(xc.dtype)


def wkv7_scan(r, decay, k, v, kk, kka):
    B, S, H, N = r.shape

    def step(state, inp):
        r_t, w_t, k_t, v_t, kk_t, b_t = inp
        sa = jnp.einsum('bhvk,bhk->bhv', state, -kk_t)
        state = (state * w_t[:, :, None, :] + sa[..., None] * b_t[:, :, None, :]
                 + v_t[..., None] * k_t[:, :, None, :])
        y = jnp.einsum('bhvk,bhk->bhv', state, r_t)
        return state, y

    s0 = jnp.zeros((B, H, N, N), jnp.float32)
    xs = tuple(jnp.swapaxes(t, 0, 1) for t in (r, decay, k, v, kk, kka))
    _, ys = lax.scan(step, s0, xs)
    return jnp.swapaxes(ys, 0, 1)


def setup_inputs(seed: int = 0) -> dict:
    key = jax.random.key(seed)
    ks = iter(jax.random.split(key, 64))
    f32 = jnp.float32
    L = DEPTH
    Lv = DEPTH - 1

    def nrm(shape, scale):
        return scale * jax.random.normal(next(ks), shape, f32)

    def gain(shape, base=1.0):
        return base + 0.02 * jax.random.normal(next(ks), shape, f32)

    x = jax.random.normal(next(ks), (BATCH, SEQ, D_MODEL), f32)
    p = jax.random.normal(next(ks), (DEPTH, BATCH, SEQ, D_PLE), f32)

    a_target = jax.random.uniform(next(ks), (L, D_LRU), f32, 0.9, 0.999)
    s = a_target ** (1.0 / LRU_C)
    lru_lambda = jnp.log(s) - jnp.log1p(-s)

    ratio = jnp.arange(D_RWKV, dtype=f32) / (D_RWKV - 1)
    rwkv_w0 = (-5.5 + 5.0 * ratio ** 0.9)[None, :] + nrm((L, D_RWKV), 0.1)

    return {
        'x': x,
        'p': p,
        'ln_mix': gain((L, D_MODEL)),
        'w_in': nrm((L, D_MODEL, D_IN), D_MODEL ** -0.5),
        'w_in_vres': nrm((Lv, D_MODEL, LORA_V), D_MODEL ** -0.5),
        'mu_shift': jax.random.uniform(next(ks), (L, N_SHIFT), f32),
        'mu_shift_vres': jax.random.uniform(next(ks), (Lv, LORA_V), f32),
        'conv_a_w': nrm((L, LRU_CONV, D_LRU), LRU_CONV ** -0.5),
        'conv_a_b': nrm((L, D_LRU), 0.02),
        'lru_wx': nrm((L, LRU_HEADS, LRU_BLOCK, LRU_BLOCK), LRU_BLOCK ** -0.5),
        'lru_bx': nrm((L, D_LRU), 0.02),
        'lru_wa': nrm((L, LRU_HEADS, LRU_BLOCK, LRU_BLOCK), LRU_BLOCK ** -0.5),
        'lru_ba': nrm((L, D_LRU), 0.02),
        'lru_lambda': lru_lambda,
        'lru_norm': gain((L, D_LRU)),
        'rwkv_w0': rwkv_w0,
        'rwkv_w2': nrm((L, LORA_W, D_RWKV), 0.5 * LORA_W ** -0.5),
        'rwkv_a0': nrm((L, D_RWKV), 0.1),
        'rwkv_a2': nrm((L, LORA_A, D_RWKV), LORA_A ** -0.5),
        'rwkv_v0': gain((Lv, D_RWKV)),
        'rwkv_v2': nrm((Lv, LORA_V, D_RWKV), LORA_V ** -0.5),
        'rwkv_g2': nrm((L, LORA_G, D_RWKV), LORA_G ** -0.5),
        'rwkv_kk': gain((L, D_RWKV), 0.85),
        'rwkv_ka': gain((L, D_RWKV)),
        'rwkv_rk': nrm((L, RWKV_HEADS, RWKV_HEAD), 0.1),
        'rwkv_lnx_w': gain((L, D_RWKV)),
        'rwkv_lnx_b': nrm((L, D_RWKV), 0.02),
        'w_o': nrm((L, D_MIX, D_MODEL), D_MIX ** -0.5),
        'ln_ffn': gain((L, D_MODEL)),
        'w_gate': nrm((L, D_MODEL, D_FF), D_MODEL ** -0.5),
        'w_up': nrm((L, D_MODEL, D_FF), D_MODEL ** -0.5),
        'conv_f_w': nrm((L, FFN_CONV, D_FF), FFN_CONV ** -0.5),
        'conv_f_b': nrm((L, D_FF), 0.02),
        'w_down': nrm((L, D_FF, D_MODEL), D_FF ** -0.5),
        'ln_ple': gain((L, D_MODEL)),
        'w_ple_gate': nrm((L, D_MODEL, D_MODEL), D_MODEL ** -0.5),
        'w_ple_proj': nrm((L, D_PLE, D_MODEL), D_PLE ** -0.5),
        'ln_ple_post': gain((L, D_MODEL)),
        'ln_final': gain((D_MODEL,)),
    }


def reference(x, p, ln_mix, w_in, w_in_vres, mu_shift, mu_shift_vres, conv_a_w, conv_a_b,
              lru_wx, lru_bx, lru_wa, lru_ba, lru_lambda, lru_norm,
              rwkv_w0, rwkv_w2, rwkv_a0, rwkv_a2, rwkv_v0, rwkv_v2, rwkv_g2,
              rwkv_kk, rwkv_ka, rwkv_rk, rwkv_lnx_w, rwkv_lnx_b, w_o,
              ln_ffn, w_gate, w_up, conv_f_w, conv_f_b, w_down,
              ln_ple, w_ple_gate, w_ple_proj, ln_ple_post, ln_final):
    B, S, _ = x.shape
    H, N = RWKV_HEADS, RWKV_HEAD

    def heads(t):
        return t.reshape(B, S, H, N)

    h = x
    v_first = None
    for i in range(DEPTH):
        u = rmsnorm(h, ln_mix[i])
        if i == 0:
            w_cat, mu = w_in[0], mu_shift[0]
        else:
            w_cat = jnp.concatenate([w_in[i], w_in_vres[i - 1]], axis=1)
            mu = jnp.concatenate([mu_shift[i], mu_shift_vres[i - 1]], axis=0)
        z = u @ w_cat

        xb = causal_dwconv(z[..., :D_LRU], conv_a_w[i], conv_a_b[i])
        yb = jax.nn.gelu(z[..., D_LRU:2 * D_LRU])
        hl = rg_lru(xb, lru_wx[i], lru_bx[i], lru_wa[i], lru_ba[i], lru_lambda[i])
        out_a = rmsnorm(hl * yb, lru_norm[i])

        zr = z[..., 2 * D_LRU:]
        zr = zr + (token_shift(zr) - zr) * mu
        r = zr[..., :D_RWKV]
        k = zr[..., D_RWKV:2 * D_RWKV]
        v = zr[..., 2 * D_RWKV:3 * D_RWKV]
        o = 3 * D_RWKV
        wl = zr[..., o:o + LORA_W]
        o += LORA_W
        al = zr[..., o:o + LORA_A]
        o += LORA_A
        gl = zr[..., o:o + LORA_G]

        w_log = -jax.nn.softplus(-(rwkv_w0[i] + jnp.tanh(wl) @ rwkv_w2[i])) - 0.5
        decay = jnp.exp(-jnp.exp(w_log.astype(jnp.float32)))
        a = jax.nn.sigmoid(rwkv_a0[i] + al @ rwkv_a2[i])
        g = jax.nn.sigmoid(gl) @ rwkv_g2[i]
        if i == 0:
            v_first = v
        else:
            vl = zr[..., N_SHIFT:]
            v = v + (v_first - v) * jax.nn.sigmoid(rwkv_v0[i - 1] + vl @ rwkv_v2[i - 1])

        kk = heads((k * rwkv_kk[i]).astype(jnp.float32))
        kk = kk / jnp.maximum(jnp.sqrt(jnp.sum(kk * kk, axis=-1, keepdims=True)), 1e-12)
        k = k * (1.0 + (a - 1.0) * rwkv_ka[i])
        rh, kh, vh = heads(r), heads(k), heads(v)
        ah = heads(a.astype(jnp.float32))
        y = wkv7_scan(rh.astype(jnp.float32), heads(decay), kh.astype(jnp.float32),
                      vh.astype(jnp.float32), kk, kk * ah)
        mean = jnp.mean(y, axis=-1, keepdims=True)
        var = jnp.mean(jnp.square(y - mean), axis=-1, keepdims=True)
        yn = ((y - mean) * lax.rsqrt(var + LNX_EPS)).reshape(B, S, D_RWKV).astype(x.dtype)
        yn = yn * rwkv_lnx_w[i] + rwkv_lnx_b[i]
        bonus = (jnp.sum(rh * kh * rwkv_rk[i], axis=-1, keepdims=True) * vh).reshape(B, S, D_RWKV)
        out_b = (yn + bonus) * g

        h = h + jnp.concatenate([out_a, out_b], axis=-1) @ w_o[i]

        u = rmsnorm(h, ln_ffn[i])
        gate = causal_dwconv(u @ w_gate[i], conv_f_w[i], conv_f_b[i])
        h = h + (jax.nn.gelu(gate) * (u @ w_up[i])) @ w_down[i]

        u = rmsnorm(h, ln_ple[i])
        e = jax.nn.sigmoid(u @ w_ple_gate[i]) * (p[i] @ w_ple_proj[i])
        h = h + rmsnorm(e, ln_ple_post[i])

    return rmsnorm(h, ln_final)
```

```python
import contextlib
import numpy as np
import concourse.bass as bass
import concourse.mybir as mybir
from concourse.bass_utils import run_bass_kernel_spmd

F32 = mybir.dt.float32
BF16 = mybir.dt.bfloat16
ALU = mybir.AluOpType
AF = mybir.ActivationFunctionType
ISZ = {F32: 4, BF16: 2}

NCORES = 8
D = 2048
S = 2048
T = 512
NT = S // T
HT = 256
CH = 64
NCH = HT // CH
DL = 1024
DFF = 6144
DPLE = 256
C0 = float(np.exp(-0.5))
RMS_EPS = 1e-6
LNX_EPS = 64e-5
NWT = 219


class Op:
    __slots__ = ("stream", "fn", "ctr", "inc", "deps", "signal", "count", "gidx", "waits")

    def __init__(self, stream, fn, ctr, inc):
        self.stream = stream
        self.fn = fn
        self.ctr = ctr
        self.inc = inc
        self.deps = {}
        self.signal = False
        self.count = 0
        self.waits = []


class Prog:
    def __init__(self, nc):
        self.nc = nc
        self.ops = []
        self.acc = {}
        self.psum_names = set()
        self.dram_names = set()
        self.last_on_ctr = {}

    def _region(self, ap):
        t = ap.tensor
        name = t.name
        isz = ISZ[ap.dtype]
        apl = [(int(s), int(c)) for (s, c) in ap.ap]
        off = int(ap.offset) * isz
        if name in self.dram_names:
            ext = sum((c - 1) * abs(s) for s, c in apl) * isz
            return name, 0, 1, off, off + ext + isz
        row = 1
        for s in t.shape[1:]:
            row *= int(s)
        row *= isz
        pstep, pcnt = apl[0]
        p0 = off // row
        f0 = off % row
        p1 = p0 + 1 if pstep == 0 else p0 + (pcnt - 1) * ((pstep * isz) // row) + 1
        ext = sum((c - 1) * abs(s) for s, c in apl[1:]) * isz
        return name, p0, p1, f0, f0 + ext + isz

    def _add_dep(self, op, dep):
        if dep is op:
            return
        if dep.stream == "pe" and op.stream == "pe":
            return
        cur = op.deps.get(dep.ctr)
        if cur is None or dep.gidx > cur.gidx:
            op.deps[dep.ctr] = dep

    def _track(self, op, ap, is_write):
        name, p0, p1, f0, f1 = self._region(ap)
        recs = self.acc.get(name)
        if recs is None:
            recs = self.acc[name] = []
        is_psum = name in self.psum_names
        keep = []
        for r in recs:
            rp0, rp1, rf0, rf1, rop, rw = r
            if is_psum and rop.stream != op.stream:
                self._add_dep(op, rop)
                continue
            overlap = rp0 < p1 and p0 < rp1 and rf0 < f1 and f0 < rf1
            if overlap and (is_write or rw):
                self._add_dep(op, rop)
            covered = rp0 >= p0 and rp1 <= p1 and rf0 >= f0 and rf1 <= f1
            if covered and (is_write or (not rw and rop.stream == op.stream)):
                continue
            keep.append(r)
        keep.append((p0, p1, f0, f1, op, is_write))
        self.acc[name] = keep

    def op(self, stream, fn, reads=(), writes=(), dma=None):
        if dma is not None:
            ctr, inc = "dma:" + str(dma), 16
        else:
            ctr, inc = stream, 1
        o = Op(stream, fn, ctr, inc)
        o.gidx = len(self.ops)
        for ap in reads:
            self._track(o, ap, False)
        for ap in writes:
            self._track(o, ap, True)
        if dma is not None:
            prev = self.last_on_ctr.get(ctr)
            if prev is not None:
                self._add_dep(o, prev)
            self.last_on_ctr[ctr] = o
        self.ops.append(o)
        return o

    def finalize(self, final_wait_ops=()):
        for o in self.ops:
            for d in o.deps.values():
                d.signal = True
        for o in final_wait_ops:
            o.signal = True
        counts = {}
        for o in self.ops:
            if o.signal:
                counts[o.ctr] = counts.get(o.ctr, 0) + o.inc
                o.count = counts[o.ctr]
        know = {}
        snap = {}
        for o in self.ops:
            k = know.setdefault(o.stream, {})
            for ctr, d in o.deps.items():
                if k.get(ctr, 0) >= d.count:
                    continue
                o.waits.append((ctr, d.count))
                for c2, v2 in snap[d.gidx].items():
                    if k.get(c2, 0) < v2:
                        k[c2] = v2
            if o.signal:
                s = dict(k)
                s[o.ctr] = max(s.get(o.ctr, 0), o.count)
                snap[o.gidx] = s
        return counts

    def emit(self, final_wait_ops=()):
        nc = self.nc
        counts = self.finalize(final_wait_ops)
        with contextlib.ExitStack() as es:
            sems = {}
            for c in sorted(counts.keys()):
                sems[c] = es.enter_context(nc.semaphore("s_" + c.replace(":", "_")))
            block = es.enter_context(nc.Block())
            by_stream = {}
            for o in self.ops:
                by_stream.setdefault(o.stream, []).append(o)

            def run(stream, eng, last=False):
                for o in by_stream.get(stream, []):
                    for ctr, val in o.waits:
                        eng.wait_ge(sems[ctr], val)
                    ins = o.fn(eng)
                    if o.signal:
                        ins.then_inc(sems[o.ctr], o.inc)
                if last:
                    for o in final_wait_ops:
                        eng.wait_ge(sems[o.ctr], o.count)

            @block.sync
            def _(e):
                run("sp", e, last=True)

            @block.tensor
            def _(e):
                run("pe", e)

            @block.scalar
            def _(e):
                run("act", e)

            @block.vector
            def _(e):
                run("dve", e)

            @block.gpsimd
            def _(e):
                run("pool", e)


VEC_FIELDS = [("ln_mix", 16), ("mu", 27), ("caw", 32), ("cab", 8), ("bx", 8), ("ba", 8), ("lam", 8),
              ("lrun", 8), ("w0", 8), ("a0", 8), ("v0", 8), ("kkw", 8), ("ka", 8), ("rk", 8),
              ("lnxw", 8), ("lnxb", 8), ("ln_ffn", 16), ("cfw", 144), ("cfb", 48), ("ln_ple", 16),
              ("ln_plep", 16), ("ln_fin", 16),
              ("omu", 27), ("omka", 8), ("sc", 8), ("tmpa", 8), ("tmpb", 8), ("tmpc", 8)]
VOFF = {}
_o = 0
for _n, _c in VEC_FIELDS:
    VOFF[_n] = _o
    _o += _c
NV = _o
NV_HOST = VOFF["omu"]

CST_IDENT, CST_ONES, CST_BONES, CST_MASK, CST_CMASK, CST_HM = 0, 128, 256, 384, 896, 1408
NCST = 1412


def _chunks(v, n):
    return np.ascontiguousarray(np.asarray(v, np.float32).reshape(n, 128).T)


def _pad128(v):
    o = np.zeros(128, np.float32)
    o[:len(v)] = v
    return o


def host_consts():
    c = np.zeros((128, NCST), np.float32)
    c[:, CST_IDENT:CST_IDENT + 128] = np.eye(128, dtype=np.float32)
    c[:, CST_ONES:CST_ONES + 128] = 1.0
    bo = np.zeros((128, 128), np.float32)
    bo[:64, :64] = 1.0
    bo[64:, 64:] = 1.0
    c[:, CST_BONES:CST_BONES + 128] = bo
    pl = (np.arange(128) % 64)[:, None]
    col = np.arange(512)[None, :]
    m = np.zeros((128, 512), np.float32)
    cl = col % 64
    m[:, 0:128] = (cl[:, 0:128] < pl)
    m[:, 128:256] = (cl[:, 128:256] > pl)
    m[:, 256:320] = (cl[:, 256:320] >= pl)
    m[:, 320:448] = (cl[:, 320:448] > pl)
    m[:, 448:512] = (cl[:, 448:512] >= pl)
    c[:, CST_MASK:CST_MASK + 512] = m
    cm = np.ones((128, T), np.float32)
    cm[:, ::CH] = 0.0
    c[:, CST_CMASK:CST_CMASK + T] = cm
    hm0 = (np.arange(128) < 64).astype(np.float32)
    c[:, CST_HM + 0] = hm0
    c[:, CST_HM + 1] = 1 - hm0
    c[:, CST_HM + 2] = -hm0
    c[:, CST_HM + 3] = -(1 - hm0)
    return c


def in_cols(l):
    base = 2 * DL
    d = {}
    for c in range(8):
        d[("xb", c)] = np.arange(128 * c, 128 * c + 128)
        d[("yb", c)] = np.arange(DL + 128 * c, DL + 128 * c + 128)
        d[("r", c)] = np.arange(base + 128 * c, base + 128 * c + 128)
        d[("k", c)] = np.arange(base + 1024 + 128 * c, base + 1024 + 128 * c + 128)
        d[("v", c)] = np.arange(base + 2048 + 128 * c, base + 2048 + 128 * c + 128)
    d[("l", 0)] = np.arange(base + 3072, base + 3200)
    d[("l", 1)] = np.arange(base + 3200, base + 3328)
    d[("l", 2)] = np.arange(base + 3328, base + 3360 + (32 if l == 1 else 0))
    return d


def weight_order(l):
    o = [("in", ("l", 0)), ("in", ("l", 1)), ("in", ("l", 2))]
    for hh in range(4):
        o += [("in", ("xb", 2 * hh)), ("in", ("xb", 2 * hh + 1)), ("in", ("yb", 2 * hh)), ("in", ("yb", 2 * hh + 1))]
    for j in range(8):
        o += [("in", ("r", j)), ("in", ("k", j)), ("in", ("v", j))]
    for c in range(16):
        o.append(("wo", c))
    for g in range(3):
        for f in range(16 * g, 16 * g + 16):
            o += [("gate", f), ("up", f)]
        for c in range(16):
            o.append(("down", c, g))
    for c in range(16):
        o.append(("pg", c))
    assert len(o) == NWT
    return o


def _wtile(wsub):
    m = wsub.shape[1]
    t = np.zeros((128, 16, 128), np.float32)
    t[:, :, :m] = wsub.reshape(16, 128, m).transpose(1, 0, 2)
    return t.reshape(128, 2048)


def host_pack(inp):
    g = {k: np.asarray(v, np.float32) for k, v in inp.items()}
    wall = np.zeros((2, NWT, 128, 2048), np.float32)
    vec = np.zeros((2, 128, NV), np.float32)
    lruw = np.zeros((2, 4, 128, 2, 2, 2, 128), np.float32)
    lora = np.zeros((2, 8, 128, 3, 128), np.float32)
    ppw = np.zeros((2, 16, 128, 2, 128), np.float32)
    for l in range(2):
        if l == 0:
            wcat = g["w_in"][0]
            mu = g["mu_shift"][0]
        else:
            wcat = np.concatenate([g["w_in"][1], g["w_in_vres"][0]], axis=1)
            mu = np.concatenate([g["mu_shift"][1], g["mu_shift_vres"][0]], axis=0)
        cols = in_cols(l)
        for n, desc in enumerate(weight_order(l)):
            kind = desc[0]
            if kind == "in":
                wall[l, n] = _wtile(wcat[:, cols[desc[1]]])
            elif kind == "wo":
                c = desc[1]
                wall[l, n] = _wtile(g["w_o"][l][:, 128 * c:128 * c + 128])
            elif kind == "gate":
                f = desc[1]
                wall[l, n] = _wtile(g["w_gate"][l][:, 128 * f:128 * f + 128])
            elif kind == "up":
                f = desc[1]
                wall[l, n] = _wtile(g["w_up"][l][:, 128 * f:128 * f + 128])
            elif kind == "down":
                c, gg = desc[1], desc[2]
                wall[l, n] = _wtile(g["w_down"][l][2048 * gg:2048 * gg + 2048, 128 * c:128 * c + 128])
            elif kind == "pg":
                c = desc[1]
                wall[l, n] = _wtile(g["w_ple_gate"][l][:, 128 * c:128 * c + 128])
        V = vec[l]

        def put(name, arr):
            V[:, VOFF[name]:VOFF[name] + arr.shape[1]] = arr
        put("ln_mix", _chunks(g["ln_mix"][l], 16))
        muz = mu - 0
        mcols = []
        for which in ("r", "k", "v"):
            for j in range(8):
                mcols.append(muz[cols[(which, j)] - 2 * DL])
        for q in range(3):
            mcols.append(_pad128(muz[cols[("l", q)] - 2 * DL]))
        put("mu", np.stack(mcols, axis=1))
        put("caw", np.concatenate([_chunks(g["conv_a_w"][l][tap], 8) for tap in range(4)], axis=1))
        put("cab", _chunks(g["conv_a_b"][l], 8))
        put("bx", _chunks(g["lru_bx"][l], 8))
        put("ba", _chunks(g["lru_ba"][l], 8))
        put("lam", _chunks(g["lru_lambda"][l], 8))
        put("lrun", _chunks(g["lru_norm"][l], 8))
        put("w0", _chunks(g["rwkv_w0"][l], 8))
        put("a0", _chunks(g["rwkv_a0"][l], 8))
        if l == 1:
            put("v0", _chunks(g["rwkv_v0"][0], 8))
        put("kkw", _chunks(g["rwkv_kk"][l], 8))
        put("ka", _chunks(g["rwkv_ka"][l], 8))
        put("rk", _chunks(g["rwkv_rk"][l].reshape(-1), 8))
        put("lnxw", _chunks(g["rwkv_lnx_w"][l], 8))
        put("lnxb", _chunks(g["rwkv_lnx_b"][l], 8))
        put("ln_ffn", _chunks(g["ln_ffn"][l], 16))
        put("cfw", np.concatenate([_chunks(g["conv_f_w"][l][tap], 48) for tap in range(3)], axis=1))
        put("cfb", _chunks(g["conv_f_b"][l], 48))
        put("ln_ple", _chunks(g["ln_ple"][l], 16))
        put("ln_plep", _chunks(g["ln_ple_post"][l], 16))
        put("ln_fin", _chunks(g["ln_final"], 16))
        for hh in range(4):
            for gi, key in enumerate(("lru_wa", "lru_wx")):
                w = g[key][l, hh]
                lruw[l, hh, :, gi] = w.reshape(2, 128, 2, 128).transpose(1, 2, 0, 3)
        for j in range(8):
            sl = slice(128 * j, 128 * j + 128)
            lora[l, j, 0:64, 0] = g["rwkv_w2"][l][:, sl]
            lora[l, j, 64:128, 0] = g["rwkv_a2"][l][:, sl]
            lora[l, j, :, 1] = g["rwkv_g2"][l][0:128, sl]
            lora[l, j, 0:32, 2] = g["rwkv_g2"][l][128:160, sl]
            if l == 1:
                lora[l, j, 32:64, 2] = g["rwkv_v2"][0][:, sl]
        for c in range(16):
            ppw[l, c] = g["w_ple_proj"][l][:, 128 * c:128 * c + 128].reshape(2, 128, 128).transpose(1, 0, 2)
    shared = {
        "wall": wall.reshape(2 * NWT, 128, 2048),
        "vec": vec[:, :, :NV_HOST].copy(),
        "lruw": lruw.reshape(2, 4, 128, 1024),
        "lora": lora.reshape(2, 8, 128, 384),
        "ppw": ppw.reshape(2, 16, 128, 256),
        "cst": host_consts(),
    }
    percore = []
    x = g["x"]
    p = g["p"]
    for b in range(NCORES):
        percore.append({
            "xT": np.ascontiguousarray(x[b].T).reshape(16, 128, S),
            "pT": np.ascontiguousarray(p[:, b].transpose(0, 2, 1)).reshape(2, 2, 128, S),
        })
    return shared, percore


class Builder:
    def __init__(self, n_tiles=NT, n_layers=2, debug=None):
        self.n_tiles = n_tiles
        self.n_layers = n_layers
        self.debug = debug or {}
        self.dbg_outs = {}
        nc = self.nc = bass.Bass("TRN2", target_bir_lowering=False)
        P = self.P = Prog(nc)
        self.es = contextlib.ExitStack()

        def din(name, shape):
            P.dram_names.add(name)
            return nc.dram_tensor(name, shape, F32, kind="ExternalInput").ap()
        self.d_xT = din("xT", [16, 128, S])
        self.d_pT = din("pT", [2, 2, 128, S])
        self.d_wall = din("wall", [2 * NWT, 128, 2048])
        self.d_vec = din("vec", [2, 128, NV_HOST])
        self.d_lruw = din("lruw", [2, 4, 128, 1024])
        self.d_lora = din("lora", [2, 8, 128, 384])
        self.d_ppw = din("ppw", [2, 16, 128, 256])
        self.d_cst = din("cst", [128, NCST])
        P.dram_names.add("outT")
        self.d_out = nc.dram_tensor("outT", [16, 128, S], F32, kind="ExternalOutput").ap()

        NA = 53200
        self.A = self.es.enter_context(nc.sbuf_tensor("A", [128, NA], F32))
        self.NA = NA
        self.top = 0
        self.banks = []
        for i in range(8):
            b = self.es.enter_context(nc.psum_tensor("pb%d" % i, [128, 512], F32))
            P.psum_names.add(b.name)
            self.banks.append(b)
        self._big = 0
        self._sm = 0
        self.finals = []

    def alloc(self, cols):
        o = self.top
        self.top += cols
        assert self.top <= self.NA, ("SBUF arena overflow", self.top)
        return o

    def fv(self, off, n, p0=0, p1=128):
        return self.A[p0:p1, off:off + n]

    def bv(self, off, n):
        return self.A[:, off:off + (n + 1) // 2].bitcast(BF16)

    def pbig(self):
        b = self.banks[self._big % 4]
        self._big += 1
        return b

    def psm(self):
        b = self.banks[4 + self._sm % 3]
        self._sm += 1
        return b

    def p1big(self):
        b = self.banks[self._big % 2]
        self._big += 1
        return b

    def p1sm(self):
        return self.banks[2]

    def rbank(self):
        b = self.banks[3 + self._sm % 4]
        self._sm += 1
        return b

    def mm(self, out, lhsT, rhs, start=True, stop=True):
        self.P.op("pe", lambda e: e.matmul(out, lhsT, rhs, start=start, stop=stop), reads=[lhsT, rhs], writes=[out])

    def tr(self, out, in_, ident):
        self.P.op("pe", lambda e: e.transpose(out, in_, ident), reads=[in_, ident], writes=[out])

    def act(self, out, in_, func, bias=None, scale=1.0):
        reads = [in_]
        kw = {}
        if bias is not None:
            kw["bias"] = bias
            if not isinstance(bias, float):
                reads.append(bias)
        if not isinstance(scale, float):
            reads.append(scale)
        kw["scale"] = scale
        self.P.op("act", lambda e: e.activation(out=out, in_=in_, func=func, **kw), reads=reads, writes=[out])

    def tt(self, eng, out, in0, in1, op):
        self.P.op(eng, lambda e: e.tensor_tensor(out=out, in0=in0, in1=in1, op=op), reads=[in0, in1], writes=[out])

    def ts(self, eng, out, in0, s1, s2, op0, op1=None):
        reads = [in0]
        if not isinstance(s1, float):
            reads.append(s1)
        if s2 is not None and not isinstance(s2, float):
            reads.append(s2)
        if op1 is None:
            self.P.op(eng, lambda e: e.tensor_scalar(out=out, in0=in0, scalar1=s1, scalar2=None, op0=op0),
                      reads=reads, writes=[out])
        else:
            self.P.op(eng, lambda e: e.tensor_scalar(out=out, in0=in0, scalar1=s1, scalar2=s2, op0=op0, op1=op1),
                      reads=reads, writes=[out])

    def stt(self, out, in0, scalar, in1, op0, op1):
        reads = [in0, in1]
        if not isinstance(scalar, float):
            reads.append(scalar)
        self.P.op("dve", lambda e: e.scalar_tensor_tensor(out=out, in0=in0, scalar=scalar, in1=in1, op0=op0, op1=op1),
                  reads=reads, writes=[out])

    def copy(self, eng, out, in_):
        if eng == "act":
            self.P.op("act", lambda e: e.copy(out=out, in_=in_), reads=[in_], writes=[out])
        else:
            self.P.op(eng, lambda e: e.tensor_copy(out=out, in_=in_), reads=[in_], writes=[out])

    def memset(self, eng, ap, val):
        self.P.op(eng, lambda e: e.memset(ap, val), writes=[ap])

    def recip(self, out, in_):
        self.P.op("dve", lambda e: e.reciprocal(out=out, in_=in_), reads=[in_], writes=[out])

    def scan(self, out, d0, d1, initial):
        reads = [d0, d1]
        if not isinstance(initial, float):
            reads.append(initial)
        self.P.op("dve", lambda e: e.tensor_tensor_scan(out=out, data0=d0, data1=d1, initial=initial,
                                                        op0=ALU.mult, op1=ALU.add), reads=reads, writes=[out])

    def dma(self, out, in_, key, stream="sp"):
        return self.P.op(stream, lambda e: e.dma_start(out=out, in_=in_), reads=[in_], writes=[out], dma=key)

    def dump(self, name, ap, shape):
        if name not in self.debug:
            return
        nm = "dbg_" + name
        self.P.dram_names.add(nm)
        o = self.nc.dram_tensor(nm, list(shape), F32, kind="ExternalOutput").ap()
        self.dbg_outs[nm] = shape
        self.finals.append(self.dma(o, ap, "dbg_" + name))

    def vcol(self, l, name, i=0):
        o = self.vec_off + l * NV + VOFF[name] + i
        return self.A[:, o:o + 1]

    def vcolp(self, l, name, i, p0, p1):
        o = self.vec_off + l * NV + VOFF[name] + i
        return self.A[p0:p1, o:o + 1]

    def build(self):
        A = self.A
        M, AD, SUB, MX = ALU.mult, ALU.add, ALU.subtract, ALU.max
        self.h_off = self.alloc(16 * T)
        self.u_off = self.alloc(16 * T // 2)
        self.mix_off = self.alloc(16 * T // 2)
        self.wst_off = [self.alloc(2048) for _ in range(3)]
        self.wbf_off = [self.alloc(1024) for _ in range(2)]
        self.s2d_off = self.alloc(2 * 8 * 128)
        self.vf_off = self.alloc(8 * T // 2)
        self.vec_off = self.alloc(2 * NV)
        self.cst_off = self.alloc(NCST)
        self.onesbf_off = self.alloc(64)
        self.identbf_off = self.alloc(64)
        self.bonesbf_off = self.alloc(64)
        self.cst_lru = self.alloc(2 * 8 * 3)
        self.cst_sh = self.alloc(2 * 27)
        self.cst_h = self.alloc(2 * 8)
        self.cst_ff = self.alloc(2 * 48 * 2)
        self.lora_off = [self.alloc(384) for _ in range(2)]
        self.sq_off = [self.alloc(T // 2) for _ in range(3)]
        self.rstd_off = self.alloc(T)
        self.wc_off = self.alloc(8)
        scratch = self.top

        h3 = A[:, self.h_off:self.h_off + 16 * T].rearrange("p (c t) -> p c t", t=T)
        u3 = self.bv(self.u_off, 16 * T).rearrange("p (c t) -> p c t", t=T)
        mix3 = self.bv(self.mix_off, 16 * T).rearrange("p (c t) -> p c t", t=T)
        vf3 = self.bv(self.vf_off, 8 * T).rearrange("p (c t) -> p c t", t=T)
        self.h3, self.u3, self.mix3, self.vf3 = h3, u3, mix3, vf3
        co = self.cst_off
        ident = A[:, co + CST_IDENT:co + CST_IDENT + 128]
        onesf = A[:, co + CST_ONES:co + CST_ONES + 128]
        bones = A[:, co + CST_BONES:co + CST_BONES + 128]
        maskall = A[:, co + CST_MASK:co + CST_MASK + 512]
        cmask = A[:, co + CST_CMASK:co + CST_CMASK + T]
        hm = [A[:, co + CST_HM + i:co + CST_HM + i + 1] for i in range(4)]
        onesbf = self.bv(self.onesbf_off, 128)
        self.ident, self.bones, self.onesbf = ident, bones, onesbf
        rstd = A[:, self.rstd_off:self.rstd_off + T]
        sqb = [self.bv(o, T) for o in self.sq_off]

        self.dma(A[:, co:co + NCST], self.d_cst, "cst")
        for l in range(2):
            vo = self.vec_off + l * NV
            self.dma(A[:, vo:vo + NV_HOST], self.d_vec[l], "vec")
        self.copy("dve", onesbf, onesf)
        identbf = self.bv(self.identbf_off, 128)
        self.copy("dve", identbf, ident)
        bonesbf = self.bv(self.bonesbf_off, 128)
        self.copy("dve", bonesbf, bones)
        for o, n in ((self.cst_lru, 48), (self.cst_sh, 54), (self.cst_h, 16), (self.cst_ff, 192), (self.s2d_off, 2048)):
            self.memset("pool", A[:, o:o + n], 0.0)
        for l in range(2):
            vo = self.vec_off + l * NV
            V = lambda name, n: A[:, vo + VOFF[name]:vo + VOFF[name] + n]
            self.ts("dve", V("omu", 27), V("mu", 27), -1.0, 1.0, M, AD)
            self.ts("dve", V("omka", 8), V("ka", 8), -1.0, 1.0, M, AD)
            x = V("tmpa", 8)
            self.act(x, V("lam", 8), AF.Exp, scale=-1.0)
            ln1p = V("tmpb", 8)
            self.act(ln1p, x, AF.Ln, bias=1.0)
            ser = V("tmpc", 8)
            self.ts("dve", ser, x, -0.25, 1.0 / 3.0, M, AD)
            self.tt("dve", ser, ser, x, M)
            self.ts("dve", ser, ser, -0.5, None, AD)
            self.tt("dve", ser, ser, x, M)
            self.ts("dve", ser, ser, 1.0, None, AD)
            self.tt("dve", ser, ser, x, M)
            msk = V("sc", 8)
            self.ts("dve", msk, x, 0.05, None, ALU.is_lt)
            self.tt("dve", ser, ser, ln1p, SUB)
            self.tt("dve", ser, ser, msk, M)
            self.tt("dve", ser, ser, ln1p, AD)
            self.ts("dve", V("sc", 8), ser, -8.0, None, M)

        seq = []
        for ti in range(self.n_tiles):
            for l in range(self.n_layers):
                for n in range(NWT):
                    seq.append(l * NWT + n)
        self.wseq = seq
        self.w_dma_done = 0
        self.w_cast_done = 0
        self.w_next = 0

        def w_issue_dma(i):
            slot = i % 3
            dst = A[:, self.wst_off[slot]:self.wst_off[slot] + 2048]
            self.dma(dst, self.d_wall[seq[i]], "ws%d" % slot)

        def w_issue_cast(i):
            slot = i % 3
            src = A[:, self.wst_off[slot]:self.wst_off[slot] + 2048]
            dst = self.bv(self.wbf_off[i % 2], 2048)
            eng = ("act", "act", "act", "dve")[i % 4]
            self.copy(eng, dst, src)

        def wget():
            i = self.w_next
            self.w_next += 1
            while self.w_dma_done < min(len(seq), i + 3):
                w_issue_dma(self.w_dma_done)
                self.w_dma_done += 1
            while self.w_cast_done < min(len(seq), i + 2):
                w_issue_cast(self.w_cast_done)
                self.w_cast_done += 1
            return self.bv(self.wbf_off[i % 2], 2048).rearrange("p (k m) -> p k m", m=128)
        self.wget = wget

        def proj(bank, Mo, rhs3):
            wt = wget()
            for kc in range(16):
                self.mm(bank[0:Mo, 0:T], wt[:, kc, 0:Mo], rhs3[:, kc, :], start=(kc == 0), stop=(kc == 15))

        def proj_g(bank, Mo, rhs3):
            wt = wget()
            for kc in range(16):
                self.mm(bank[0:Mo, 0:T], wt[:, kc, 0:Mo], rhs3[:, kc, :], start=(kc == 0), stop=(kc == 15))
                if kc % 4 == 3:
                    yield

        def slack(n):
            for _ in range(n):
                yield

        def rms_stats(chunks, nfeat, eps):
            bank = self.psm()
            n = len(chunks)
            for i, sap in enumerate(chunks):
                sq = sqb[i % 3]
                if i % 3 == 0:
                    self.act(sq, sap, AF.Square)
                elif i % 3 == 1:
                    self.tt("dve", sq, sap, sap, M)
                else:
                    self.tt("pool", sq, sap, sap, M)
                self.mm(bank[:, 0:T], onesbf, sq, start=(i == 0), stop=(i == n - 1))
            self.act(rstd, bank[:, 0:T], AF.Ln, bias=float(eps), scale=1.0 / nfeat)
            self.act(rstd, rstd, AF.Exp, scale=-0.5)
            return rstd

        self.top = scratch
        zt_off = [self.alloc(1 + T) for _ in range(2)]
        l40_off = self.alloc(T)
        tw_off = self.alloc(T)
        sg1_off = self.alloc(T)
        l42_off = self.alloc(T)
        mixer_mark = self.top
        self.lrug_off2 = [self.alloc(1024) for _ in range(2)]
        zx_off = [self.alloc(3 + T) for _ in range(2)]
        xc_off = [self.alloc(T) for _ in range(4)]
        xcb_off = [self.alloc(T // 2) for _ in range(4)]
        lrugb_off = [self.alloc(512) for _ in range(2)]
        yb_off = [self.alloc(T) for _ in range(4)]
        lt_off = [self.alloc(T) for _ in range(6)]
        oa_off = self.alloc(8 * T)
        lru_top = self.top
        self.top = mixer_mark
        rkv_off = [self.alloc(T) for _ in range(3)]
        full_off = {n: self.alloc(T) for n in ("sgw", "av", "kap", "kp", "bvec", "t1", "t2")}
        ltmp_off = full_off["t1"]
        g_off = [self.alloc(T // 2) for _ in range(2)]
        bvb_off = [self.alloc(T // 2) for _ in range(2)]
        NC8 = T // CH
        arm_off = self.alloc(NC8 * 192 // 2)
        bm_off = self.alloc(NC8 * 64)
        km_off = self.alloc(NC8 * 64)
        vm_off = self.alloc(NC8 * 64)
        khm_off = self.alloc(NC8 * 64)
        bhm_off = self.alloc(NC8 * 64)
        ams_off = self.alloc(NC8 * 256)
        ov_mark = self.top
        half_off = {n: self.alloc(T) for n in ("Lc", "Lex", "Winv", "Ld")}
        half_off["Wt"] = half_off["Lc"]
        half_off["Wprev"] = half_off["Lex"]
        half_off["Wend"] = half_off["Ld"]
        ov1_top = self.top
        self.top = ov_mark
        nb_off = self.alloc(NC8 * 64)
        pbuf_off = self.alloc(NC8 * 64)
        acc_off = self.alloc(NC8 * 64)
        vd_off = self.alloc(NC8 * 64)
        khd_off = self.alloc(NC8 * 64)
        bhd_off = self.alloc(NC8 * 64)
        xs_off = self.alloc(64)
        us_off = self.alloc(64)
        s2b_off = self.alloc(64)
        ys_off = self.alloc(T)
        yc_off = self.alloc(T)
        rs_off = self.alloc(T)
        rw_top = max(self.top, ov1_top)
        self.top = scratch
        hid_off = self.alloc(16 * T // 2)
        gz_off = [self.alloc(2 + T) for _ in range(2)]
        gacc_off = [self.alloc(T) for _ in range(2)]
        ffn_top = self.top
        self.top = scratch
        e_off = self.alloc(16 * T)
        self.ppw_off = [self.alloc(256) for _ in range(2)]
        self.pt_off = self.alloc(2 * T)
        sgp_off = [self.alloc(T) for _ in range(2)]
        ple_top = self.top
        self.top = scratch
        ob_off = self.alloc(16 * T)
        self.peak = max(lru_top, rw_top, ffn_top, ple_top, self.top)
        assert self.peak <= self.NA, self.peak

        FV = lambda off, n=T, p0=0, p1=128: A[p0:p1, off:off + n]

        for ti in range(self.n_tiles):
            t0 = ti * T
            first = (ti == 0)
            self.dma(h3, self.d_xT.rearrange("c p t -> p c t")[:, :, t0:t0 + T], "x")
            for l in range(self.n_layers):
                vc = lambda name, i=0, l=l: self.vcol(l, name, i)
                rs_ = rms_stats([h3[:, c, :] for c in range(16)], D, RMS_EPS)
                for c in range(16):
                    self.stt(u3[:, c, :], h3[:, c, :], vc("ln_mix", c), rs_, M, M)
                if ti == 0 and l == 0:
                    self.dump("u0", A[:, self.u_off:self.u_off + 16 * T // 2], [128, 16 * T // 2])

                def shift_evac(bank, Mo, muidx, dst, l=l, on_pool=False):
                    zt = FV(zt_off[self._zt % 2], 1 + T)
                    self._zt += 1
                    so = self.cst_sh + l * 27 + muidx
                    self.copy("pool", zt[0:Mo, 0:1], A[0:Mo, so:so + 1])
                    self.act(zt[0:Mo, 1:1 + T], bank[0:Mo, 0:T], AF.Copy)
                    self.copy("pool", A[0:Mo, so:so + 1], zt[0:Mo, T:T + 1])
                    tmp = FV(ltmp_off)
                    self.act(tmp[0:Mo, :], bank[0:Mo, 0:T], AF.Identity, scale=self.vcolp(l, "omu", muidx, 0, Mo))
                    if on_pool:
                        self.ts("pool", dst, zt[0:Mo, 0:T], self.vcolp(l, "mu", muidx, 0, Mo), 0.0, M, AD)
                        self.tt("pool", dst, dst, tmp[0:Mo, :], AD)
                    else:
                        self.stt(dst, zt[0:Mo, 0:T], self.vcolp(l, "mu", muidx, 0, Mo), tmp[0:Mo, :], M, AD)
                self._zt = 0

                def shift_evac_g(bank, Mo, muidx, dst, l=l):
                    zt = FV(zt_off[self._zt % 2], 1 + T)
                    self._zt += 1
                    so = self.cst_sh + l * 27 + muidx
                    tmp = FV(ltmp_off)
                    self.copy("pool", zt[0:Mo, 0:1], A[0:Mo, so:so + 1])
                    self.act(zt[0:Mo, 1:1 + T], bank[0:Mo, 0:T], AF.Copy)
                    self.act(tmp[0:Mo, :], bank[0:Mo, 0:T], AF.Identity, scale=self.vcolp(l, "omu", muidx, 0, Mo))
                    yield from slack(3)
                    self.copy("pool", A[0:Mo, so:so + 1], zt[0:Mo, T:T + 1])
                    self.ts("pool", dst, zt[0:Mo, 0:T], self.vcolp(l, "mu", muidx, 0, Mo), 0.0, M, AD)
                    self.tt("pool", dst, dst, tmp[0:Mo, :], AD)

                l40 = FV(l40_off)
                tw = FV(tw_off)
                sg1 = FV(sg1_off)
                l42 = FV(l42_off)
                m42 = 32 if l == 0 else 64
                for q, (Mo, dst) in enumerate(((128, l40), (128, sg1), (m42, l42[0:m42, :]))):
                    bank = self.pbig()
                    proj(bank, Mo, u3)
                    shift_evac(bank, Mo, 24 + q, dst)
                self.act(tw[0:64, :], l40[0:64, :], AF.Tanh)
                self.act(sg1, sg1, AF.Sigmoid)
                self.act(l42[0:32, :], l42[0:32, :], AF.Sigmoid)

                oa3 = A[:, oa_off:oa_off + 8 * T].rearrange("p (c t) -> p c t", t=T)
                def gen_Lp(hh, l=l):
                    lg = self.lrug_off2[hh % 2]
                    self.dma(A[:, lg:lg + 1024], self.d_lruw[l, hh], "lrug%d" % (hh % 2))
                    self.copy("pool", self.bv(lrugb_off[hh % 2], 1024), A[:, lg:lg + 1024])
                    xcs = [FV(xc_off[2 * (hh % 2)]), FV(xc_off[2 * (hh % 2) + 1])]
                    ybs = [FV(yb_off[2 * (hh % 2)]), FV(yb_off[2 * (hh % 2) + 1])]
                    for q in range(2):
                        cc = 2 * hh + q
                        bank = self.p1big()
                        yield from proj_g(bank, 128, u3)
                        zx = FV(zx_off[q], 3 + T)
                        so = self.cst_lru + (l * 8 + cc) * 3
                        self.copy("pool", zx[:, 0:3], A[:, so:so + 3])
                        self.act(zx[:, 3:3 + T], bank[:, 0:T], AF.Copy)
                        self.copy("pool", A[:, so:so + 3], zx[:, T:T + 3])
                        yield
                        xc = xcs[q]
                        self.ts("dve", xc, zx[:, 3:3 + T], vc("caw", 3 * 8 + cc), vc("cab", cc), M, AD)
                        for tap in range(3):
                            self.stt(xc, zx[:, tap:tap + T], vc("caw", tap * 8 + cc), xc, M, AD)
                            yield
                        self.copy("pool", self.bv(xcb_off[2 * (hh % 2) + q], T), xc)
                    for q in range(2):
                        bank = self.p1big()
                        yield from proj_g(bank, 128, u3)
                        self.act(ybs[q], bank[:, 0:T], AF.Gelu_apprx_tanh)
                        yield

                def gen_Lc(hh, l=l, first=first):
                    lg = self.lrug_off2[hh % 2]
                    lrug = self.bv(lrugb_off[hh % 2], 1024).rearrange("p (g j k m) -> p g j k m", g=2, j=2, k=2)
                    xcs = [FV(xc_off[2 * (hh % 2)]), FV(xc_off[2 * (hh % 2) + 1])]
                    xcbs = [self.bv(xcb_off[2 * (hh % 2)], T), self.bv(xcb_off[2 * (hh % 2) + 1], T)]
                    ybs = [FV(yb_off[2 * (hh % 2)]), FV(yb_off[2 * (hh % 2) + 1])]
                    for jj in range(2):
                        j = 2 * hh + jj
                        ba_, bx_ = self.rbank(), self.rbank()
                        for kc in range(2):
                            self.mm(ba_[:, 0:T], lrug[:, 0, jj, kc, :], xcbs[kc], start=(kc == 0), stop=(kc == 1))
                        yield
                        for kc in range(2):
                            self.mm(bx_[:, 0:T], lrug[:, 1, jj, kc, :], xcbs[kc], start=(kc == 0), stop=(kc == 1))
                        yield
                        ga, gx, aa, m2, bi, hl = [FV(o) for o in lt_off]
                        self.act(ga, ba_[:, 0:T], AF.Sigmoid, bias=vc("ba", j))
                        self.act(gx, bx_[:, 0:T], AF.Sigmoid, bias=vc("bx", j))
                        yield
                        self.act(aa, ga, AF.Exp, scale=vc("sc", j))
                        self.tt("pool", m2, aa, aa, M)
                        self.ts("pool", m2, m2, -1.0, 1.0, M, AD)
                        yield
                        self.act(m2, m2, AF.Sqrt)
                        if first:
                            self.memset("pool", m2[:, 0:1], 1.0)
                        self.tt("dve", bi, xcs[jj], gx, M)
                        yield
                        self.tt("dve", bi, bi, m2, M)
                        so = self.cst_h + l * 8 + j
                        self.scan(hl, aa, bi, A[:, so:so + 1])
                        yield
                        self.copy("pool", A[:, so:so + 1], hl[:, T - 1:T])
                        self.tt("dve", oa3[:, j, :], hl, ybs[jj], M)
                        yield

                def interleave(ga_, gb_):
                    alive_a, alive_b = ga_ is not None, gb_ is not None
                    while alive_a or alive_b:
                        if alive_a:
                            try:
                                next(ga_)
                            except StopIteration:
                                alive_a = False
                        if alive_b:
                            try:
                                next(gb_)
                            except StopIteration:
                                alive_b = False
                interleave(gen_Lp(0), None)
                for hh in range(4):
                    interleave(gen_Lc(hh), gen_Lp(hh + 1) if hh < 3 else None)
                rs_ = rms_stats([oa3[:, j, :] for j in range(8)], DL, RMS_EPS)
                for j in range(8):
                    self.stt(mix3[:, j, :], oa3[:, j, :], vc("lrun", j), rs_, M, M)
                if ti == 0 and l == 0:
                    self.dump("oa0", A[:, oa_off:oa_off + 8 * T], [128, 8 * T])

                def gen_P1(j, l=l, ti=ti):
                    lo_off = self.lora_off[j % 2]
                    self.dma(A[:, lo_off:lo_off + 384], self.d_lora[l, j], "lora%d" % (j % 2))
                    lo = A[:, lo_off:lo_off + 384].rearrange("p (q m) -> p q m", m=128)
                    r_ = FV(rkv_off[0])
                    k_ = FV(rkv_off[1])
                    v_ = FV(rkv_off[2])
                    SL = 3
                    for which, dst in enumerate((r_, k_, v_)):
                        bank = self.p1big()
                        yield from proj_g(bank, 128, u3)
                        yield from slack(SL)
                        yield from shift_evac_g(bank, 128, which * 8 + j, dst)
                        yield
                    sgw, av, kap, kp, bvec, t1, t2 = [FV(full_off[n]) for n in
                                                      ("sgw", "av", "kap", "kp", "bvec", "t1", "t2")]
                    gb = self.bv(g_off[j % 2], T)
                    bvv = self.bv(bvb_off[j % 2], T)
                    t1b = self.bv(full_off["t1"], T)
                    bk = self.p1sm()
                    self.mm(bk[:, 0:T], lo[0:64, 0, :], tw[0:64, :])
                    yield from slack(SL)
                    self.act(sgw, bk[:, 0:T], AF.Sigmoid, bias=vc("w0", j))
                    yield
                    bk = self.p1sm()
                    self.mm(bk[:, 0:T], lo[64:128, 0, :], l40[64:128, :])
                    yield from slack(SL)
                    self.act(av, bk[:, 0:T], AF.Sigmoid, bias=vc("a0", j))
                    yield
                    if l == 0:
                        self.copy("pool", vf3[:, j, :], v_)
                    if l == 1:
                        bk = self.p1sm()
                        self.mm(bk[:, 0:T], lo[32:64, 2, :], l42[32:64, :])
                        yield from slack(SL)
                        self.act(t2, bk[:, 0:T], AF.Sigmoid, bias=vc("v0", j))
                        self.tt("pool", t1, vf3[:, j, :], v_, SUB)
                        yield from slack(SL)
                        self.tt("pool", t1, t1, t2, M)
                        self.tt("pool", v_, v_, t1, AD)
                        yield
                    bk = self.p1sm()
                    self.mm(bk[:, 0:T], lo[:, 1, :], sg1, start=True, stop=False)
                    self.mm(bk[:, 0:T], lo[0:32, 2, :], l42[0:32, :], start=False, stop=True)
                    yield from slack(SL)
                    self.act(gb, bk[:, 0:T], AF.Copy)
                    yield
                    self.act(t1b, k_, AF.Square, scale=vc("kkw", j))
                    self.ts("pool", kap, k_, vc("kkw", j), 0.0, M, AD)
                    self.ts("pool", t2, av, vc("ka", j), vc("omka", j), M, AD)
                    yield from slack(SL)
                    bk = self.p1sm()
                    self.mm(bk[:, 0:T], bonesbf, t1b)
                    self.tt("pool", kp, k_, t2, M)
                    self.ts("pool", t2, r_, vc("rk", j), 0.0, M, AD)
                    yield from slack(SL)
                    rn = FV(full_off["t1"])
                    self.act(rn, bk[:, 0:T], AF.Ln, bias=1e-18)
                    self.act(rn, rn, AF.Exp, scale=-0.5)
                    prb = self.bv(full_off["bvec"], T)
                    self.tt("pool", prb, t2, kp, M)
                    yield from slack(SL)
                    bk = self.p1sm()
                    self.mm(bk[:, 0:T], bonesbf, prb)
                    self.tt("pool", kap, kap, rn, M)
                    yield from slack(SL)
                    self.tt("pool", bvec, av, kap, M)
                    self.tt("dve", bvv, bk[:, 0:T], v_, M)
                    yield
                    if ti == 0 and l == 0 and j == 0:
                        self.dump("r0", r_, [128, T])
                        self.dump("kp0", kp, [128, T])
                        self.dump("v0", v_, [128, T])
                        self.dump("kap0", kap, [128, T])
                        self.dump("sgw0", sgw, [128, T])
                        self.dump("av0", av, [128, T])

                def gen_rest(j, part, l=l, ti=ti):
                    S2 = A[:, self.s2d_off + (l * 8 + j) * 128:self.s2d_off + (l * 8 + j) * 128 + 128]
                    r_ = FV(rkv_off[0])
                    v_ = FV(rkv_off[2])
                    sgw, kap, kp, bvec = [FV(full_off[n]) for n in ("sgw", "kap", "kp", "bvec")]
                    gb = self.bv(g_off[j % 2], T)
                    bvv = self.bv(bvb_off[j % 2], T)
                    NC8 = T // CH
                    Lc, Lex, Wt, Winv, Wprev, Wend, Ld = [FV(half_off[n]) for n in
                                                          ("Lc", "Lex", "Wt", "Winv", "Wprev", "Wend", "Ld")]
                    c3 = lambda ap: ap.rearrange("p (c t) -> p c t", t=CH)
                    wc = A[:, self.wc_off:self.wc_off + NC8]
                    bview = lambda off, x: self.bv(off, NC8 * x).rearrange("p (c x) -> p c x", x=x)
                    arm = bview(arm_off, 192)
                    bm, km, vm, khm, bhm = [bview(o, 128) for o in (bm_off, km_off, vm_off, khm_off, bhm_off)]
                    ams = bview(ams_off, 512)
                    bY = self.banks[7]
                    if part == "p2":
                        self.scan(Lc, cmask, sgw, 0.0)
                        yield
                        self.tt("pool", Lex, Lc, sgw, SUB)
                        self.tt("pool", c3(Ld), c3(Lc)[:, :, CH - 1:CH].broadcast_to([128, NC8, CH]), c3(Lc), SUB)
                        self.act(Winv, Lc, AF.Exp, scale=C0)
                        yield
                        self.act(Wt, Lc, AF.Exp, scale=-C0)
                        yield
                        self.act(Wprev, Lex, AF.Exp, scale=-C0)
                        self.act(Wend, Ld, AF.Exp, scale=-C0)
                        self.copy("pool", wc, c3(Wt)[:, :, CH - 1])
                        yield
                        pks = [FV(full_off["t2"]), FV(full_off["t1"])]
                        plan = ((kap, Wprev, arm, 2), (bvec, Winv, bm, 0), (kp, Winv, km, 0), (kp, Wend, khm, 0), (bvec, Wend, bhm, 0))
                        for qi, (xa, xw, dstv, mo) in enumerate(plan):
                            pk = pks[qi % 2]
                            self.tt("dve", pk, xa, xw, M)
                            yield
                            self.act(dstv[:, :, 0:64], c3(pk), AF.Identity, scale=hm[mo + 0])
                            self.ts("pool", dstv[:, :, 64:128], c3(pk), hm[mo + 1], 0.0, M, AD)
                            yield
                        for hp in range(2):
                            cs = slice(hp * 64, hp * 64 + 64)
                            self.ts("pool", vm[:, :, cs], c3(v_), hm[hp], 0.0, M, AD)
                        self.tt("dve", arm[:, :, 128:192], c3(r_), c3(Wt), M)
                        yield
                        return
                    if part == "o":
                        ys, yc, rs2 = FV(ys_off), FV(yc_off), FV(rs_off)
                        self.act(ys, bY[:, 0:T], AF.Copy)
                        if ti == 0 and l == 0 and j == 0:
                            self.dump("y0", ys[:, 0:HT], [128, HT])
                        yield
                        bk = self.rbank()
                        self.mm(bk[:, 0:T], bones, ys)
                        self.stt(yc, bk[:, 0:T], -1.0 / 64.0, ys, M, AD)
                        yield
                        ysb = self.bv(ys_off, T)
                        self.act(ysb, yc, AF.Square)
                        bk = self.rbank()
                        self.mm(bk[:, 0:T], bonesbf, ysb)
                        yield
                        self.act(rs2, bk[:, 0:T], AF.Ln, bias=float(LNX_EPS), scale=1.0 / 64.0)
                        self.act(rs2, rs2, AF.Exp, scale=-0.5)
                        yield
                        self.tt("dve", yc, yc, rs2, M)
                        self.ts("dve", yc, yc, vc("lnxw", j), vc("lnxb", j), M, AD)
                        yield
                        self.tt("pool", yc, yc, bvv, AD)
                        self.tt("dve", mix3[:, 8 + j, :], yc, gb, M)
                        yield
                        return
                    for c in range(NC8):
                        bk = self.rbank()
                        self.mm(bk[:, 0:128], arm[:, c, 0:128], bm[:, c, :])
                        self.mm(bk[:, 128:320], bm[:, c, :], arm[:, c, 0:192])
                        self.mm(bk[:, 320:512], km[:, c, :], arm[:, c, 0:192])
                        yield
                        self.tt("dve", ams[:, c, :], bk[:, :], maskall, M)
                        yield
                    vd, khd, bhd = [bview(o, 128) for o in (vd_off, khd_off, bhd_off)]
                    for src, off in ((vm, vd_off), (khm, khd_off), (bhm, bhd_off)):
                        bk = self.rbank()
                        bkb = bk[:, :].bitcast(BF16)
                        for c in range(NC8):
                            self.tr(bkb[:, c * 128:(c + 1) * 128], src[:, c, :], identbf)
                        self.act(self.bv(off, NC8 * 128), bkb, AF.Copy)
                        yield
                    acc = bview(acc_off, 128)
                    nbv, pbv = bview(nb_off, 128), bview(pbuf_off, 128)
                    identb3 = identbf.rearrange("p (o x) -> p o x", o=1).broadcast_to([128, NC8, 128])
                    self.tt("dve", acc, ams[:, :, 128:256], identb3, AD)
                    Ncur, Pcur = ams[:, :, 0:128], ams[:, :, 128:256]
                    for lev in range(1, 6):
                        bN = [self.rbank(), self.rbank()]
                        for c in range(NC8):
                            self.mm(bN[c // 4][:, (c % 4) * 128:(c % 4 + 1) * 128], Pcur[:, c, :], Ncur[:, c, :])
                        yield
                        if lev < 5:
                            bP = [self.rbank(), self.rbank()]
                            for c in range(NC8):
                                self.mm(bP[c // 4][:, (c % 4) * 128:(c % 4 + 1) * 128], Ncur[:, c, :], Pcur[:, c, :])
                        yield
                        for hb in range(2):
                            self.act(nbv[:, 4 * hb:4 * hb + 4, :], bN[hb][:, :].rearrange("p (c x) -> p c x", x=128), AF.Copy)
                        if lev < 5:
                            for hb in range(2):
                                self.copy("dve", pbv[:, 4 * hb:4 * hb + 4, :], bP[hb][:, :].rearrange("p (c x) -> p c x", x=128))
                            Pcur = pbv
                        Ncur = nbv
                        yield
                        yield
                        bA = [self.rbank(), self.rbank()]
                        for c in range(NC8):
                            self.mm(bA[c // 4][:, (c % 4) * 128:(c % 4 + 1) * 128], Ncur[:, c, :], acc[:, c, :])
                        for hb in range(2):
                            self.tt("dve", acc[:, 4 * hb:4 * hb + 4, :], bA[hb][:, :].rearrange("p (c x) -> p c x", x=128),
                                    acc[:, 4 * hb:4 * hb + 4, :], AD)
                        yield
                    bY = self.banks[7]
                    Xs = self.bv(xs_off, 128)
                    Us = self.bv(us_off, 128)
                    S2b = self.bv(s2b_off, 128)
                    self.copy("dve", S2b, S2)
                    for c in range(NC8):
                        bX = self.rbank()
                        self.mm(bX[:, 0:128], arm[:, c, 0:128], S2b, start=True, stop=False)
                        self.mm(bX[:, 0:128], ams[:, c, 320:448], vd[:, c, :], start=False, stop=True)
                        self.act(Xs, bX[:, 0:128], AF.Copy)
                        yield
                        self.mm(bX[:, 128:256], acc[:, c, :], Xs)
                        self.act(Us, bX[:, 128:256], AF.Copy)
                        yield
                        yo = bY[:, c * CH:(c + 1) * CH]
                        self.mm(yo, S2b, arm[:, c, 128:192], start=True, stop=False)
                        self.mm(yo, Us, ams[:, c, 256:320], start=False, stop=False)
                        self.mm(yo, vd[:, c, :], ams[:, c, 448:512], start=False, stop=True)
                        yield
                        self.mm(bX[:, 256:384], khd[:, c, :], vd[:, c, :], start=True, stop=False)
                        self.mm(bX[:, 256:384], bhd[:, c, :], Us, start=False, stop=True)
                        yield
                        self.stt(S2b, S2, wc[:, c:c + 1], bX[:, 256:384], M, AD)
                        self.stt(S2, S2, wc[:, c:c + 1], bX[:, 256:384], M, AD)
                        yield
                def drain(g):
                    for _ in g:
                        pass
                drain(gen_P1(0))
                drain(gen_rest(0, "p2"))
                for j in range(8):
                    interleave(gen_rest(j, "chain"), gen_P1(j + 1) if j < 7 else None)
                    interleave(gen_rest(j, "o"), gen_rest(j + 1, "p2") if j < 7 else None)
                if ti == 0 and l == 0:
                    self.dump("mix0", A[:, self.mix_off:self.mix_off + 16 * T // 2], [128, 16 * T // 2])

                for c in range(16):
                    bank = self.pbig()
                    proj(bank, 128, mix3)
                    self.tt("dve", h3[:, c, :], bank[:, 0:T], h3[:, c, :], AD)
                if ti == 0 and l == 0:
                    self.dump("hmix0", A[:, self.h_off:self.h_off + 16 * T], [128, 16 * T])

                rs_ = rms_stats([h3[:, c, :] for c in range(16)], D, RMS_EPS)
                for c in range(16):
                    self.stt(u3[:, c, :], h3[:, c, :], vc("ln_ffn", c), rs_, M, M)
                hid3 = self.bv(hid_off, 16 * T).rearrange("p (c t) -> p c t", t=T)
                for g in range(3):
                    for fi in range(16):
                        f = 16 * g + fi
                        bg = self.pbig()
                        proj(bg, 128, u3)
                        bu = self.pbig()
                        proj(bu, 128, u3)
                        gz = FV(gz_off[fi % 2], 2 + T)
                        ga = FV(gacc_off[fi % 2])
                        so = self.cst_ff + (l * 48 + f) * 2
                        self.copy("pool", gz[:, 0:2], A[:, so:so + 2])
                        self.act(gz[:, 2:2 + T], bg[:, 0:T], AF.Copy)
                        self.copy("pool", A[:, so:so + 2], gz[:, T:T + 2])
                        self.ts("dve", ga, gz[:, 2:2 + T], vc("cfw", 2 * 48 + f), vc("cfb", f), M, AD)
                        self.stt(ga, gz[:, 1:1 + T], vc("cfw", 48 + f), ga, M, AD)
                        self.stt(ga, gz[:, 0:T], vc("cfw", f), ga, M, AD)
                        self.act(ga, ga, AF.Gelu_apprx_tanh)
                        self.tt("dve", hid3[:, fi, :], bu[:, 0:T], ga, M)
                    for c in range(16):
                        bank = self.pbig()
                        proj(bank, 128, hid3)
                        self.tt("dve", h3[:, c, :], bank[:, 0:T], h3[:, c, :], AD)
                if ti == 0 and l == 0:
                    self.dump("hffn0", A[:, self.h_off:self.h_off + 16 * T], [128, 16 * T])

                rs_ = rms_stats([h3[:, c, :] for c in range(16)], D, RMS_EPS)
                for c in range(16):
                    self.stt(u3[:, c, :], h3[:, c, :], vc("ln_ple", c), rs_, M, M)
                pt = A[:, self.pt_off:self.pt_off + 2 * T].rearrange("p (k t) -> p k t", t=T)
                self.dma(pt, self.d_pT[l].rearrange("k p t -> p k t")[:, :, t0:t0 + T], "p")
                e3 = A[:, e_off:e_off + 16 * T].rearrange("p (c t) -> p c t", t=T)
                for c in range(16):
                    po = self.ppw_off[c % 2]
                    self.dma(A[:, po:po + 256], self.d_ppw[l, c], "ppw%d" % (c % 2))
                    pw = A[:, po:po + 256].rearrange("p (k m) -> p k m", m=128)
                    bank = self.pbig()
                    proj(bank, 128, u3)
                    sg = FV(sgp_off[c % 2])
                    self.act(sg, bank[:, 0:T], AF.Sigmoid)
                    bk = self.psm()
                    for kc in range(2):
                        self.mm(bk[:, 0:T], pw[:, kc, :], pt[:, kc, :], start=(kc == 0), stop=(kc == 1))
                    self.tt("dve", e3[:, c, :], bk[:, 0:T], sg, M)
                rs_ = rms_stats([e3[:, c, :] for c in range(16)], D, RMS_EPS)
                for c in range(16):
                    self.tt("pool", e3[:, c, :], e3[:, c, :], rs_, M)
                    self.stt(h3[:, c, :], e3[:, c, :], vc("ln_plep", c), h3[:, c, :], M, AD)
                if ti == 0 and l == 0:
                    self.dump("hple0", A[:, self.h_off:self.h_off + 16 * T], [128, 16 * T])

            rs_ = rms_stats([h3[:, c, :] for c in range(16)], D, RMS_EPS)
            ob3 = A[:, ob_off:ob_off + 16 * T].rearrange("p (c t) -> p c t", t=T)
            for c in range(16):
                self.stt(ob3[:, c, :], h3[:, c, :], self.vcol(0, "ln_fin", c), rs_, M, M)
            self.finals.append(self.dma(self.d_out.rearrange("c p t -> p c t")[:, :, t0:t0 + T], ob3, "out"))

        self.P.emit(final_wait_ops=self.finals)
        return self.nc


_CACHE = {}


def kernel(**inputs):
    shared, percore = host_pack(inputs)
    if "nc" not in _CACHE:
        b = Builder()
        _CACHE["nc"] = b.build()
    nc = _CACHE["nc"]
    in_maps = []
    for c in range(NCORES):
        m = dict(shared)
        m.update(percore[c])
        in_maps.append(m)
    res = run_bass_kernel_spmd(nc, in_maps, core_ids=list(range(NCORES)))
    out = np.empty((NCORES, S, D), np.float32)
    for c in range(NCORES):
        o = np.asarray(res.results[c]["outT"]).reshape(D, S)
        out[c] = o.T
    return out
```

```python
import contextlib
import numpy as np
import concourse.bass as bass
import concourse.mybir as mybir
from concourse.bass_utils import run_bass_kernel_spmd

F32 = mybir.dt.float32
BF16 = mybir.dt.bfloat16
ALU = mybir.AluOpType
AF = mybir.ActivationFunctionType
ISZ = {F32: 4, BF16: 2}

NCORES = 8
D = 2048
S = 2048
T = 512
NT = S // T
HT = 256
CH = 64
NCH = HT // CH
DL = 1024
DFF = 6144
DPLE = 256
C0 = float(np.exp(-0.5))
RMS_EPS = 1e-6
LNX_EPS = 64e-5
NWT = 219


class Op:
    __slots__ = ("stream", "fn", "ctr", "inc", "deps", "signal", "count", "gidx", "waits")

    def __init__(self, stream, fn, ctr, inc):
        self.stream = stream
        self.fn = fn
        self.ctr = ctr
        self.inc = inc
        self.deps = {}
        self.signal = False
        self.count = 0
        self.waits = []


class Prog:
    def __init__(self, nc):
        self.nc = nc
        self.ops = []
        self.acc = {}
        self.psum_names = set()
        self.dram_names = set()
        self.last_on_ctr = {}

    def _region(self, ap):
        t = ap.tensor
        name = t.name
        isz = ISZ[ap.dtype]
        apl = [(int(s), int(c)) for (s, c) in ap.ap]
        off = int(ap.offset) * isz
        if name in self.dram_names:
            ext = sum((c - 1) * abs(s) for s, c in apl) * isz
            return name, 0, 1, off, off + ext + isz
        row = 1
        for s in t.shape[1:]:
            row *= int(s)
        row *= isz
        pstep, pcnt = apl[0]
        p0 = off // row
        f0 = off % row
        p1 = p0 + 1 if pstep == 0 else p0 + (pcnt - 1) * ((pstep * isz) // row) + 1
        ext = sum((c - 1) * abs(s) for s, c in apl[1:]) * isz
        return name, p0, p1, f0, f0 + ext + isz

    def _add_dep(self, op, dep):
        if dep is op:
            return
        if dep.stream == "pe" and op.stream == "pe":
            return
        cur = op.deps.get(dep.ctr)
        if cur is None or dep.gidx > cur.gidx:
            op.deps[dep.ctr] = dep

    def _track(self, op, ap, is_write):
        name, p0, p1, f0, f1 = self._region(ap)
        recs = self.acc.get(name)
        if recs is None:
            recs = self.acc[name] = []
        is_psum = name in self.psum_names
        keep = []
        for r in recs:
            rp0, rp1, rf0, rf1, rop, rw = r
            if is_psum and rop.stream != op.stream:
                self._add_dep(op, rop)
                continue
            overlap = rp0 < p1 and p0 < rp1 and rf0 < f1 and f0 < rf1
            if overlap and (is_write or rw):
                self._add_dep(op, rop)
            covered = rp0 >= p0 and rp1 <= p1 and rf0 >= f0 and rf1 <= f1
            if covered and (is_write or (not rw and rop.stream == op.stream)):
                continue
            keep.append(r)
        keep.append((p0, p1, f0, f1, op, is_write))
        self.acc[name] = keep

    def op(self, stream, fn, reads=(), writes=(), dma=None):
        if dma is not None:
            ctr, inc = "dma:" + str(dma), 16
        else:
            ctr, inc = stream, 1
        o = Op(stream, fn, ctr, inc)
        o.gidx = len(self.ops)
        for ap in reads:
            self._track(o, ap, False)
        for ap in writes:
            self._track(o, ap, True)
        if dma is not None:
            prev = self.last_on_ctr.get(ctr)
            if prev is not None:
                self._add_dep(o, prev)
            self.last_on_ctr[ctr] = o
        self.ops.append(o)
        return o

    def finalize(self, final_wait_ops=()):
        for o in self.ops:
            for d in o.deps.values():
                d.signal = True
        for o in final_wait_ops:
            o.signal = True
        counts = {}
        for o in self.ops:
            if o.signal:
                counts[o.ctr] = counts.get(o.ctr, 0) + o.inc
                o.count = counts[o.ctr]
        know = {}
        snap = {}
        for o in self.ops:
            k = know.setdefault(o.stream, {})
            for ctr, d in o.deps.items():
                if k.get(ctr, 0) >= d.count:
                    continue
                o.waits.append((ctr, d.count))
                for c2, v2 in snap[d.gidx].items():
                    if k.get(c2, 0) < v2:
                        k[c2] = v2
            if o.signal:
                s = dict(k)
                s[o.ctr] = max(s.get(o.ctr, 0), o.count)
                snap[o.gidx] = s
        return counts

    def emit(self, final_wait_ops=()):
        nc = self.nc
        counts = self.finalize(final_wait_ops)
        with contextlib.ExitStack() as es:
            sems = {}
            for c in sorted(counts.keys()):
                sems[c] = es.enter_context(nc.semaphore("s_" + c.replace(":", "_")))
            block = es.enter_context(nc.Block())
            by_stream = {}
            for o in self.ops:
                by_stream.setdefault(o.stream, []).append(o)

            def run(stream, eng, last=False):
                for o in by_stream.get(stream, []):
                    for ctr, val in o.waits:
                        eng.wait_ge(sems[ctr], val)
                    ins = o.fn(eng)
                    if o.signal:
                        ins.then_inc(sems[o.ctr], o.inc)
                if last:
                    for o in final_wait_ops:
                        eng.wait_ge(sems[o.ctr], o.count)

            @block.sync
            def _(e):
                run("sp", e, last=True)

            @block.tensor
            def _(e):
                run("pe", e)

            @block.scalar
            def _(e):
                run("act", e)

            @block.vector
            def _(e):
                run("dve", e)

            @block.gpsimd
            def _(e):
                run("pool", e)


VEC_FIELDS = [("ln_mix", 16), ("mu", 27), ("caw", 32), ("cab", 8), ("bx", 8), ("ba", 8), ("lam", 8),
              ("lrun", 8), ("w0", 8), ("a0", 8), ("v0", 8), ("kkw", 8), ("ka", 8), ("rk", 8),
              ("lnxw", 8), ("lnxb", 8), ("ln_ffn", 16), ("cfw", 144), ("cfb", 48), ("ln_ple", 16),
              ("ln_plep", 16), ("ln_fin", 16),
              ("omu", 27), ("omka", 8), ("sc", 8), ("tmpa", 8), ("tmpb", 8), ("tmpc", 8)]
VOFF = {}
_o = 0
for _n, _c in VEC_FIELDS:
    VOFF[_n] = _o
    _o += _c
NV = _o
NV_HOST = VOFF["omu"]

CST_IDENT, CST_ONES, CST_BONES, CST_MASK, CST_CMASK, CST_HM = 0, 128, 256, 384, 896, 1408
NCST = 1412


def _chunks(v, n):
    return np.ascontiguousarray(np.asarray(v, np.float32).reshape(n, 128).T)


def _pad128(v):
    o = np.zeros(128, np.float32)
    o[:len(v)] = v
    return o


def host_consts():
    c = np.zeros((128, NCST), np.float32)
    c[:, CST_IDENT:CST_IDENT + 128] = np.eye(128, dtype=np.float32)
    c[:, CST_ONES:CST_ONES + 128] = 1.0
    bo = np.zeros((128, 128), np.float32)
    bo[:64, :64] = 1.0
    bo[64:, 64:] = 1.0
    c[:, CST_BONES:CST_BONES + 128] = bo
    pl = (np.arange(128) % 64)[:, None]
    col = np.arange(512)[None, :]
    m = np.zeros((128, 512), np.float32)
    cl = col % 64
    m[:, 0:128] = (cl[:, 0:128] < pl)
    m[:, 128:256] = (cl[:, 128:256] > pl)
    m[:, 256:320] = (cl[:, 256:320] >= pl)
    m[:, 320:448] = (cl[:, 320:448] > pl)
    m[:, 448:512] = (cl[:, 448:512] >= pl)
    c[:, CST_MASK:CST_MASK + 512] = m
    cm = np.ones((128, T), np.float32)
    cm[:, ::CH] = 0.0
    c[:, CST_CMASK:CST_CMASK + T] = cm
    hm0 = (np.arange(128) < 64).astype(np.float32)
    c[:, CST_HM + 0] = hm0
    c[:, CST_HM + 1] = 1 - hm0
    c[:, CST_HM + 2] = -hm0
    c[:, CST_HM + 3] = -(1 - hm0)
    return c


def in_cols(l):
    base = 2 * DL
    d = {}
    for c in range(8):
        d[("xb", c)] = np.arange(128 * c, 128 * c + 128)
        d[("yb", c)] = np.arange(DL + 128 * c, DL + 128 * c + 128)
        d[("r", c)] = np.arange(base + 128 * c, base + 128 * c + 128)
        d[("k", c)] = np.arange(base + 1024 + 128 * c, base + 1024 + 128 * c + 128)
        d[("v", c)] = np.arange(base + 2048 + 128 * c, base + 2048 + 128 * c + 128)
    d[("l", 0)] = np.arange(base + 3072, base + 3200)
    d[("l", 1)] = np.arange(base + 3200, base + 3328)
    d[("l", 2)] = np.arange(base + 3328, base + 3360 + (32 if l == 1 else 0))
    return d


def weight_order(l):
    o = [("in", ("l", 0)), ("in", ("l", 1)), ("in", ("l", 2))]
    for hh in range(4):
        o += [("in", ("xb", 2 * hh)), ("in", ("xb", 2 * hh + 1)), ("in", ("yb", 2 * hh)), ("in", ("yb", 2 * hh + 1))]
    for j in range(8):
        o += [("in", ("r", j)), ("in", ("k", j)), ("in", ("v", j))]
    for c in range(16):
        o.append(("wo", c))
    for g in range(3):
        for f in range(16 * g, 16 * g + 16):
            o += [("gate", f), ("up", f)]
        for c in range(16):
            o.append(("down", c, g))
    for c in range(16):
        o.append(("pg", c))
    assert len(o) == NWT
    return o


def _wtile(wsub):
    m = wsub.shape[1]
    t = np.zeros((128, 16, 128), np.float32)
    t[:, :, :m] = wsub.reshape(16, 128, m).transpose(1, 0, 2)
    return t.reshape(128, 2048)


def host_pack(inp):
    g = {k: np.asarray(v, np.float32) for k, v in inp.items()}
    wall = np.zeros((2, NWT, 128, 2048), np.float32)
    vec = np.zeros((2, 128, NV), np.float32)
    lruw = np.zeros((2, 4, 128, 2, 2, 2, 128), np.float32)
    lora = np.zeros((2, 8, 128, 3, 128), np.float32)
    ppw = np.zeros((2, 16, 128, 2, 128), np.float32)
    for l in range(2):
        if l == 0:
            wcat = g["w_in"][0]
            mu = g["mu_shift"][0]
        else:
            wcat = np.concatenate([g["w_in"][1], g["w_in_vres"][0]], axis=1)
            mu = np.concatenate([g["mu_shift"][1], g["mu_shift_vres"][0]], axis=0)
        cols = in_cols(l)
        for n, desc in enumerate(weight_order(l)):
            kind = desc[0]
            if kind == "in":
                wall[l, n] = _wtile(wcat[:, cols[desc[1]]])
            elif kind == "wo":
                c = desc[1]
                wall[l, n] = _wtile(g["w_o"][l][:, 128 * c:128 * c + 128])
            elif kind == "gate":
                f = desc[1]
                wall[l, n] = _wtile(g["w_gate"][l][:, 128 * f:128 * f + 128])
            elif kind == "up":
                f = desc[1]
                wall[l, n] = _wtile(g["w_up"][l][:, 128 * f:128 * f + 128])
            elif kind == "down":
                c, gg = desc[1], desc[2]
                wall[l, n] = _wtile(g["w_down"][l][2048 * gg:2048 * gg + 2048, 128 * c:128 * c + 128])
            elif kind == "pg":
                c = desc[1]
                wall[l, n] = _wtile(g["w_ple_gate"][l][:, 128 * c:128 * c + 128])
        V = vec[l]

        def put(name, arr):
            V[:, VOFF[name]:VOFF[name] + arr.shape[1]] = arr
        put("ln_mix", _chunks(g["ln_mix"][l], 16))
        muz = mu - 0
        mcols = []
        for which in ("r", "k", "v"):
            for j in range(8):
                mcols.append(muz[cols[(which, j)] - 2 * DL])
        for q in range(3):
            mcols.append(_pad128(muz[cols[("l", q)] - 2 * DL]))
        put("mu", np.stack(mcols, axis=1))
        put("caw", np.concatenate([_chunks(g["conv_a_w"][l][tap], 8) for tap in range(4)], axis=1))
        put("cab", _chunks(g["conv_a_b"][l], 8))
        put("bx", _chunks(g["lru_bx"][l], 8))
        put("ba", _chunks(g["lru_ba"][l], 8))
        put("lam", _chunks(g["lru_lambda"][l], 8))
        put("lrun", _chunks(g["lru_norm"][l], 8))
        put("w0", _chunks(g["rwkv_w0"][l], 8))
        put("a0", _chunks(g["rwkv_a0"][l], 8))
        if l == 1:
            put("v0", _chunks(g["rwkv_v0"][0], 8))
        put("kkw", _chunks(g["rwkv_kk"][l], 8))
        put("ka", _chunks(g["rwkv_ka"][l], 8))
        put("rk", _chunks(g["rwkv_rk"][l].reshape(-1), 8))
        put("lnxw", _chunks(g["rwkv_lnx_w"][l], 8))
        put("lnxb", _chunks(g["rwkv_lnx_b"][l], 8))
        put("ln_ffn", _chunks(g["ln_ffn"][l], 16))
        put("cfw", np.concatenate([_chunks(g["conv_f_w"][l][tap], 48) for tap in range(3)], axis=1))
        put("cfb", _chunks(g["conv_f_b"][l], 48))
        put("ln_ple", _chunks(g["ln_ple"][l], 16))
        put("ln_plep", _chunks(g["ln_ple_post"][l], 16))
        put("ln_fin", _chunks(g["ln_final"], 16))
        for hh in range(4):
            for gi, key in enumerate(("lru_wa", "lru_wx")):
                w = g[key][l, hh]
                lruw[l, hh, :, gi] = w.reshape(2, 128, 2, 128).transpose(1, 2, 0, 3)
        for j in range(8):
            sl = slice(128 * j, 128 * j + 128)
            lora[l, j, 0:64, 0] = g["rwkv_w2"][l][:, sl]
            lora[l, j, 64:128, 0] = g["rwkv_a2"][l][:, sl]
            lora[l, j, :, 1] = g["rwkv_g2"][l][0:128, sl]
            lora[l, j, 0:32, 2] = g["rwkv_g2"][l][128:160, sl]
            if l == 1:
                lora[l, j, 32:64, 2] = g["rwkv_v2"][0][:, sl]
        for c in range(16):
            ppw[l, c] = g["w_ple_proj"][l][:, 128 * c:128 * c + 128].reshape(2, 128, 128).transpose(1, 0, 2)
    shared = {
        "wall": wall.reshape(2 * NWT, 128, 2048),
        "vec": vec[:, :, :NV_HOST].copy(),
        "lruw": lruw.reshape(2, 4, 128, 1024),
        "lora": lora.reshape(2, 8, 128, 384),
        "ppw": ppw.reshape(2, 16, 128, 256),
        "cst": host_consts(),
    }
    percore = []
    x = g["x"]
    p = g["p"]
    for b in range(NCORES):
        percore.append({
            "xT": np.ascontiguousarray(x[b].T).reshape(16, 128, S),
            "pT": np.ascontiguousarray(p[:, b].transpose(0, 2, 1)).reshape(2, 2, 128, S),
        })
    return shared, percore


class Builder:
    def __init__(self, n_tiles=NT, n_layers=2, debug=None):
        self.n_tiles = n_tiles
        self.n_layers = n_layers
        self.debug = debug or {}
        self.dbg_outs = {}
        nc = self.nc = bass.Bass("TRN2", target_bir_lowering=False)
        P = self.P = Prog(nc)
        self.es = contextlib.ExitStack()

        def din(name, shape):
            P.dram_names.add(name)
            return nc.dram_tensor(name, shape, F32, kind="ExternalInput").ap()
        self.d_xT = din("xT", [16, 128, S])
        self.d_pT = din("pT", [2, 2, 128, S])
        self.d_wall = din("wall", [2 * NWT, 128, 2048])
        self.d_vec = din("vec", [2, 128, NV_HOST])
        self.d_lruw = din("lruw", [2, 4, 128, 1024])
        self.d_lora = din("lora", [2, 8, 128, 384])
        self.d_ppw = din("ppw", [2, 16, 128, 256])
        self.d_cst = din("cst", [128, NCST])
        P.dram_names.add("outT")
        self.d_out = nc.dram_tensor("outT", [16, 128, S], F32, kind="ExternalOutput").ap()

        NA = 53200
        self.A = self.es.enter_context(nc.sbuf_tensor("A", [128, NA], F32))
        self.NA = NA
        self.top = 0
        self.banks = []
        for i in range(8):
            b = self.es.enter_context(nc.psum_tensor("pb%d" % i, [128, 512], F32))
            P.psum_names.add(b.name)
            self.banks.append(b)
        self._big = 0
        self._sm = 0
        self.finals = []

    def alloc(self, cols):
        o = self.top
        self.top += cols
        assert self.top <= self.NA, ("SBUF arena overflow", self.top)
        return o

    def fv(self, off, n, p0=0, p1=128):
        return self.A[p0:p1, off:off + n]

    def bv(self, off, n):
        return self.A[:, off:off + (n + 1) // 2].bitcast(BF16)

    def pbig(self):
        b = self.banks[self._big % 4]
        self._big += 1
        return b

    def psm(self):
        b = self.banks[4 + self._sm % 3]
        self._sm += 1
        return b

    def p1big(self):
        b = self.banks[self._big % 2]
        self._big += 1
        return b

    def p1sm(self):
        return self.banks[2]

    def rbank(self):
        b = self.banks[3 + self._sm % 4]
        self._sm += 1
        return b

    def mm(self, out, lhsT, rhs, start=True, stop=True):
        self.P.op("pe", lambda e: e.matmul(out, lhsT, rhs, start=start, stop=stop), reads=[lhsT, rhs], writes=[out])

    def tr(self, out, in_, ident):
        self.P.op("pe", lambda e: e.transpose(out, in_, ident), reads=[in_, ident], writes=[out])

    def act(self, out, in_, func, bias=None, scale=1.0):
        reads = [in_]
        kw = {}
        if bias is not None:
            kw["bias"] = bias
            if not isinstance(bias, float):
                reads.append(bias)
        if not isinstance(scale, float):
            reads.append(scale)
        kw["scale"] = scale
        self.P.op("act", lambda e: e.activation(out=out, in_=in_, func=func, **kw), reads=reads, writes=[out])

    def tt(self, eng, out, in0, in1, op):
        self.P.op(eng, lambda e: e.tensor_tensor(out=out, in0=in0, in1=in1, op=op), reads=[in0, in1], writes=[out])

    def ts(self, eng, out, in0, s1, s2, op0, op1=None):
        reads = [in0]
        if not isinstance(s1, float):
            reads.append(s1)
        if s2 is not None and not isinstance(s2, float):
            reads.append(s2)
        if op1 is None:
            self.P.op(eng, lambda e: e.tensor_scalar(out=out, in0=in0, scalar1=s1, scalar2=None, op0=op0),
                      reads=reads, writes=[out])
        else:
            self.P.op(eng, lambda e: e.tensor_scalar(out=out, in0=in0, scalar1=s1, scalar2=s2, op0=op0, op1=op1),
                      reads=reads, writes=[out])

    def stt(self, out, in0, scalar, in1, op0, op1):
        reads = [in0, in1]
        if not isinstance(scalar, float):
            reads.append(scalar)
        self.P.op("dve", lambda e: e.scalar_tensor_tensor(out=out, in0=in0, scalar=scalar, in1=in1, op0=op0, op1=op1),
                  reads=reads, writes=[out])

    def copy(self, eng, out, in_):
        if eng == "act":
            self.P.op("act", lambda e: e.copy(out=out, in_=in_), reads=[in_], writes=[out])
        else:
            self.P.op(eng, lambda e: e.tensor_copy(out=out, in_=in_), reads=[in_], writes=[out])

    def memset(self, eng, ap, val):
        self.P.op(eng, lambda e: e.memset(ap, val), writes=[ap])

    def recip(self, out, in_):
        self.P.op("dve", lambda e: e.reciprocal(out=out, in_=in_), reads=[in_], writes=[out])

    def scan(self, out, d0, d1, initial):
        reads = [d0, d1]
        if not isinstance(initial, float):
            reads.append(initial)
        self.P.op("dve", lambda e: e.tensor_tensor_scan(out=out, data0=d0, data1=d1, initial=initial,
                                                        op0=ALU.mult, op1=ALU.add), reads=reads, writes=[out])

    def dma(self, out, in_, key, stream="sp"):
        return self.P.op(stream, lambda e: e.dma_start(out=out, in_=in_), reads=[in_], writes=[out], dma=key)

    def dump(self, name, ap, shape):
        if name not in self.debug:
            return
        nm = "dbg_" + name
        self.P.dram_names.add(nm)
        o = self.nc.dram_tensor(nm, list(shape), F32, kind="ExternalOutput").ap()
        self.dbg_outs[nm] = shape
        self.finals.append(self.dma(o, ap, "dbg_" + name))

    def vcol(self, l, name, i=0):
        o = self.vec_off + l * NV + VOFF[name] + i
        return self.A[:, o:o + 1]

    def vcolp(self, l, name, i, p0, p1):
        o = self.vec_off + l * NV + VOFF[name] + i
        return self.A[p0:p1, o:o + 1]

    def build(self):
        A = self.A
        M, AD, SUB, MX = ALU.mult, ALU.add, ALU.subtract, ALU.max
        self.h_off = self.alloc(16 * T)
        self.u_off = self.alloc(16 * T // 2)
        self.mix_off = self.alloc(16 * T // 2)
        self.wst_off = [self.alloc(2048) for _ in range(3)]
        self.wbf_off = [self.alloc(1024) for _ in range(2)]
        self.s2d_off = self.alloc(2 * 8 * 128)
        self.vf_off = self.alloc(8 * T // 2)
        self.vec_off = self.alloc(2 * NV)
        self.cst_off = self.alloc(NCST)
        self.onesbf_off = self.alloc(64)
        self.identbf_off = self.alloc(64)
        self.bonesbf_off = self.alloc(64)
        self.cst_lru = self.alloc(2 * 8 * 3)
        self.cst_sh = self.alloc(2 * 27)
        self.cst_h = self.alloc(2 * 8)
        self.cst_ff = self.alloc(2 * 48 * 2)
        self.lora_off = [self.alloc(384) for _ in range(2)]
        self.sq_off = [self.alloc(T // 2) for _ in range(3)]
        self.rstd_off = self.alloc(T)
        self.wc_off = self.alloc(8)
        scratch = self.top

        h3 = A[:, self.h_off:self.h_off + 16 * T].rearrange("p (c t) -> p c t", t=T)
        u3 = self.bv(self.u_off, 16 * T).rearrange("p (c t) -> p c t", t=T)
        mix3 = self.bv(self.mix_off, 16 * T).rearrange("p (c t) -> p c t", t=T)
        vf3 = self.bv(self.vf_off, 8 * T).rearrange("p (c t) -> p c t", t=T)
        self.h3, self.u3, self.mix3, self.vf3 = h3, u3, mix3, vf3
        co = self.cst_off
        ident = A[:, co + CST_IDENT:co + CST_IDENT + 128]
        onesf = A[:, co + CST_ONES:co + CST_ONES + 128]
        bones = A[:, co + CST_BONES:co + CST_BONES + 128]
        maskall = A[:, co + CST_MASK:co + CST_MASK + 512]
        cmask = A[:, co + CST_CMASK:co + CST_CMASK + T]
        hm = [A[:, co + CST_HM + i:co + CST_HM + i + 1] for i in range(4)]
        onesbf = self.bv(self.onesbf_off, 128)
        self.ident, self.bones, self.onesbf = ident, bones, onesbf
        rstd = A[:, self.rstd_off:self.rstd_off + T]
        sqb = [self.bv(o, T) for o in self.sq_off]

        self.dma(A[:, co:co + NCST], self.d_cst, "cst")
        for l in range(2):
            vo = self.vec_off + l * NV
            self.dma(A[:, vo:vo + NV_HOST], self.d_vec[l], "vec")
        self.copy("dve", onesbf, onesf)
        identbf = self.bv(self.identbf_off, 128)
        self.copy("dve", identbf, ident)
        bonesbf = self.bv(self.bonesbf_off, 128)
        self.copy("dve", bonesbf, bones)
        for o, n in ((self.cst_lru, 48), (self.cst_sh, 54), (self.cst_h, 16), (self.cst_ff, 192), (self.s2d_off, 2048)):
            self.memset("pool", A[:, o:o + n], 0.0)
        for l in range(2):
            vo = self.vec_off + l * NV
            V = lambda name, n: A[:, vo + VOFF[name]:vo + VOFF[name] + n]
            self.ts("dve", V("omu", 27), V("mu", 27), -1.0, 1.0, M, AD)
            self.ts("dve", V("omka", 8), V("ka", 8), -1.0, 1.0, M, AD)
            x = V("tmpa", 8)
            self.act(x, V("lam", 8), AF.Exp, scale=-1.0)
            ln1p = V("tmpb", 8)
            self.act(ln1p, x, AF.Ln, bias=1.0)
            ser = V("tmpc", 8)
            self.ts("dve", ser, x, -0.25, 1.0 / 3.0, M, AD)
            self.tt("dve", ser, ser, x, M)
            self.ts("dve", ser, ser, -0.5, None, AD)
            self.tt("dve", ser, ser, x, M)
            self.ts("dve", ser, ser, 1.0, None, AD)
            self.tt("dve", ser, ser, x, M)
            msk = V("sc", 8)
            self.ts("dve", msk, x, 0.05, None, ALU.is_lt)
            self.tt("dve", ser, ser, ln1p, SUB)
            self.tt("dve", ser, ser, msk, M)
            self.tt("dve", ser, ser, ln1p, AD)
            self.ts("dve", V("sc", 8), ser, -8.0, None, M)

        seq = []
        for ti in range(self.n_tiles):
            for l in range(self.n_layers):
                for n in range(NWT):
                    seq.append(l * NWT + n)
        self.wseq = seq
        self.w_dma_done = 0
        self.w_cast_done = 0
        self.w_next = 0

        def w_issue_dma(i):
            slot = i % 3
            dst = A[:, self.wst_off[slot]:self.wst_off[slot] + 2048]
            self.dma(dst, self.d_wall[seq[i]], "ws%d" % slot)

        def w_issue_cast(i):
            slot = i % 3
            src = A[:, self.wst_off[slot]:self.wst_off[slot] + 2048]
            dst = self.bv(self.wbf_off[i % 2], 2048)
            eng = ("act", "act", "act", "dve")[i % 4]
            self.copy(eng, dst, src)

        def wget():
            i = self.w_next
            self.w_next += 1
            while self.w_dma_done < min(len(seq), i + 3):
                w_issue_dma(self.w_dma_done)
                self.w_dma_done += 1
            while self.w_cast_done < min(len(seq), i + 2):
                w_issue_cast(self.w_cast_done)
                self.w_cast_done += 1
            return self.bv(self.wbf_off[i % 2], 2048).rearrange("p (k m) -> p k m", m=128)
        self.wget = wget

        def proj(bank, Mo, rhs3):
            wt = wget()
            for kc in range(16):
                self.mm(bank[0:Mo, 0:T], wt[:, kc, 0:Mo], rhs3[:, kc, :], start=(kc == 0), stop=(kc == 15))

        def proj_g(bank, Mo, rhs3):
            wt = wget()
            for kc in range(16):
                self.mm(bank[0:Mo, 0:T], wt[:, kc, 0:Mo], rhs3[:, kc, :], start=(kc == 0), stop=(kc == 15))
                if kc % 4 == 3:
                    yield

        def slack(n):
            for _ in range(n):
                yield

        def rms_stats(chunks, nfeat, eps):
            bank = self.psm()
            n = len(chunks)
            for i, sap in enumerate(chunks):
                sq = sqb[i % 3]
                if i % 3 == 0:
                    self.act(sq, sap, AF.Square)
                elif i % 3 == 1:
                    self.tt("dve", sq, sap, sap, M)
                else:
                    self.tt("pool", sq, sap, sap, M)
                self.mm(bank[:, 0:T], onesbf, sq, start=(i == 0), stop=(i == n - 1))
            self.act(rstd, bank[:, 0:T], AF.Ln, bias=float(eps), scale=1.0 / nfeat)
            self.act(rstd, rstd, AF.Exp, scale=-0.5)
            return rstd

        self.top = scratch
        zt_off = [self.alloc(1 + T) for _ in range(2)]
        l40_off = self.alloc(T)
        tw_off = self.alloc(T)
        sg1_off = self.alloc(T)
        l42_off = self.alloc(T)
        mixer_mark = self.top
        self.lrug_off2 = [self.alloc(1024) for _ in range(2)]
        zx_off = [self.alloc(3 + T) for _ in range(2)]
        xc_off = [self.alloc(T) for _ in range(4)]
        xcb_off = [self.alloc(T // 2) for _ in range(4)]
        lrugb_off = [self.alloc(512) for _ in range(2)]
        yb_off = [self.alloc(T) for _ in range(4)]
        lt_off = [self.alloc(T) for _ in range(6)]
        oa_off = self.alloc(8 * T)
        lru_top = self.top
        self.top = mixer_mark
        rkv_off = [self.alloc(T) for _ in range(3)]
        full_off = {n: self.alloc(T) for n in ("sgw", "av", "kap", "kp", "bvec", "t1", "t2")}
        ltmp_off = full_off["t1"]
        g_off = [self.alloc(T // 2) for _ in range(2)]
        bvb_off = [self.alloc(T // 2) for _ in range(2)]
        NC8 = T // CH
        arm_off = self.alloc(NC8 * 192 // 2)
        bm_off = self.alloc(NC8 * 64)
        km_off = self.alloc(NC8 * 64)
        vm_off = self.alloc(NC8 * 64)
        khm_off = self.alloc(NC8 * 64)
        bhm_off = self.alloc(NC8 * 64)
        ams_off = self.alloc(NC8 * 256)
        ov_mark = self.top
        half_off = {n: self.alloc(T) for n in ("Lc", "Lex", "Winv", "Ld")}
        half_off["Wt"] = half_off["Lc"]
        half_off["Wprev"] = half_off["Lex"]
        half_off["Wend"] = half_off["Ld"]
        ov1_top = self.top
        self.top = ov_mark
        nb_off = self.alloc(NC8 * 64)
        pbuf_off = self.alloc(NC8 * 64)
        acc_off = self.alloc(NC8 * 64)
        vd_off = self.alloc(NC8 * 64)
        khd_off = self.alloc(NC8 * 64)
        bhd_off = self.alloc(NC8 * 64)
        xs_off = self.alloc(64)
        us_off = self.alloc(64)
        s2b_off = self.alloc(64)
        ys_off = self.alloc(T)
        yc_off = self.alloc(T)
        rs_off = self.alloc(T)
        rw_top = max(self.top, ov1_top)
        self.top = scratch
        hid_off = self.alloc(16 * T // 2)
        gz_off = [self.alloc(2 + T) for _ in range(2)]
        gacc_off = [self.alloc(T) for _ in range(2)]
        ffn_top = self.top
        self.top = scratch
        e_off = self.alloc(16 * T)
        self.ppw_off = [self.alloc(256) for _ in range(2)]
        self.pt_off = self.alloc(2 * T)
        ptb_off = self.alloc(T)
        ppwb_off = [self.alloc(128) for _ in range(2)]
        sgp_off = [self.alloc(T) for _ in range(2)]
        ple_top = self.top
        self.top = scratch
        ob_off = self.alloc(16 * T)
        self.peak = max(lru_top, rw_top, ffn_top, ple_top, self.top)
        assert self.peak <= self.NA, self.peak

        FV = lambda off, n=T, p0=0, p1=128: A[p0:p1, off:off + n]

        for ti in range(self.n_tiles):
            t0 = ti * T
            first = (ti == 0)
            self.dma(h3, self.d_xT.rearrange("c p t -> p c t")[:, :, t0:t0 + T], "x")
            for l in range(self.n_layers):
                vc = lambda name, i=0, l=l: self.vcol(l, name, i)
                rs_ = rms_stats([h3[:, c, :] for c in range(16)], D, RMS_EPS)
                for c in range(16):
                    self.stt(u3[:, c, :], h3[:, c, :], vc("ln_mix", c), rs_, M, M)
                if ti == 0 and l == 0:
                    self.dump("u0", A[:, self.u_off:self.u_off + 16 * T // 2], [128, 16 * T // 2])

                def shift_evac(bank, Mo, muidx, dst, l=l, on_pool=False):
                    zt = FV(zt_off[self._zt % 2], 1 + T)
                    self._zt += 1
                    so = self.cst_sh + l * 27 + muidx
                    self.copy("pool", zt[0:Mo, 0:1], A[0:Mo, so:so + 1])
                    self.act(zt[0:Mo, 1:1 + T], bank[0:Mo, 0:T], AF.Copy)
                    self.copy("pool", A[0:Mo, so:so + 1], zt[0:Mo, T:T + 1])
                    tmp = FV(ltmp_off)
                    self.act(tmp[0:Mo, :], bank[0:Mo, 0:T], AF.Identity, scale=self.vcolp(l, "omu", muidx, 0, Mo))
                    if on_pool:
                        self.ts("pool", dst, zt[0:Mo, 0:T], self.vcolp(l, "mu", muidx, 0, Mo), 0.0, M, AD)
                        self.tt("pool", dst, dst, tmp[0:Mo, :], AD)
                    else:
                        self.stt(dst, zt[0:Mo, 0:T], self.vcolp(l, "mu", muidx, 0, Mo), tmp[0:Mo, :], M, AD)
                self._zt = 0

                def shift_evac_g(bank, Mo, muidx, dst, l=l):
                    zt = FV(zt_off[self._zt % 2], 1 + T)
                    self._zt += 1
                    so = self.cst_sh + l * 27 + muidx
                    tmp = FV(ltmp_off)
                    self.copy("pool", zt[0:Mo, 0:1], A[0:Mo, so:so + 1])
                    self.act(zt[0:Mo, 1:1 + T], bank[0:Mo, 0:T], AF.Copy)
                    self.act(tmp[0:Mo, :], bank[0:Mo, 0:T], AF.Identity, scale=self.vcolp(l, "omu", muidx, 0, Mo))
                    yield from slack(3)
                    self.copy("pool", A[0:Mo, so:so + 1], zt[0:Mo, T:T + 1])
                    self.ts("pool", dst, zt[0:Mo, 0:T], self.vcolp(l, "mu", muidx, 0, Mo), 0.0, M, AD)
                    self.tt("pool", dst, dst, tmp[0:Mo, :], AD)

                l40 = FV(l40_off)
                tw = FV(tw_off)
                sg1 = FV(sg1_off)
                l42 = FV(l42_off)
                m42 = 32 if l == 0 else 64
                for q, (Mo, dst) in enumerate(((128, l40), (128, sg1), (m42, l42[0:m42, :]))):
                    bank = self.pbig()
                    proj(bank, Mo, u3)
                    shift_evac(bank, Mo, 24 + q, dst)
                self.act(tw[0:64, :], l40[0:64, :], AF.Tanh)
                self.act(sg1, sg1, AF.Sigmoid)
                self.act(l42[0:32, :], l42[0:32, :], AF.Sigmoid)

                oa3 = A[:, oa_off:oa_off + 8 * T].rearrange("p (c t) -> p c t", t=T)
                def gen_Lp(hh, l=l):
                    lg = self.lrug_off2[hh % 2]
                    self.dma(A[:, lg:lg + 1024], self.d_lruw[l, hh], "lrug%d" % (hh % 2))
                    self.copy("pool", self.bv(lrugb_off[hh % 2], 1024), A[:, lg:lg + 1024])
                    xcs = [FV(xc_off[2 * (hh % 2)]), FV(xc_off[2 * (hh % 2) + 1])]
                    ybs = [FV(yb_off[2 * (hh % 2)]), FV(yb_off[2 * (hh % 2) + 1])]
                    for q in range(2):
                        cc = 2 * hh + q
                        bank = self.p1big()
                        yield from proj_g(bank, 128, u3)
                        zx = FV(zx_off[q], 3 + T)
                        so = self.cst_lru + (l * 8 + cc) * 3
                        self.copy("pool", zx[:, 0:3], A[:, so:so + 3])
                        self.act(zx[:, 3:3 + T], bank[:, 0:T], AF.Copy)
                        self.copy("pool", A[:, so:so + 3], zx[:, T:T + 3])
                        yield
                        xc = xcs[q]
                        self.ts("dve", xc, zx[:, 3:3 + T], vc("caw", 3 * 8 + cc), vc("cab", cc), M, AD)
                        for tap in range(3):
                            self.stt(xc, zx[:, tap:tap + T], vc("caw", tap * 8 + cc), xc, M, AD)
                            yield
                        self.copy("pool", self.bv(xcb_off[2 * (hh % 2) + q], T), xc)
                    for q in range(2):
                        bank = self.p1big()
                        yield from proj_g(bank, 128, u3)
                        self.act(ybs[q], bank[:, 0:T], AF.Gelu_apprx_tanh)
                        yield

                def gen_Lc(hh, l=l, first=first):
                    lg = self.lrug_off2[hh % 2]
                    lrug = self.bv(lrugb_off[hh % 2], 1024).rearrange("p (g j k m) -> p g j k m", g=2, j=2, k=2)
                    xcs = [FV(xc_off[2 * (hh % 2)]), FV(xc_off[2 * (hh % 2) + 1])]
                    xcbs = [self.bv(xcb_off[2 * (hh % 2)], T), self.bv(xcb_off[2 * (hh % 2) + 1], T)]
                    ybs = [FV(yb_off[2 * (hh % 2)]), FV(yb_off[2 * (hh % 2) + 1])]
                    for jj in range(2):
                        j = 2 * hh + jj
                        ba_, bx_ = self.rbank(), self.rbank()
                        for kc in range(2):
                            self.mm(ba_[:, 0:T], lrug[:, 0, jj, kc, :], xcbs[kc], start=(kc == 0), stop=(kc == 1))
                        yield
                        for kc in range(2):
                            self.mm(bx_[:, 0:T], lrug[:, 1, jj, kc, :], xcbs[kc], start=(kc == 0), stop=(kc == 1))
                        yield
                        ga, gx, aa, m2, bi, hl = [FV(o) for o in lt_off]
                        self.act(ga, ba_[:, 0:T], AF.Sigmoid, bias=vc("ba", j))
                        self.act(gx, bx_[:, 0:T], AF.Sigmoid, bias=vc("bx", j))
                        yield
                        self.act(aa, ga, AF.Exp, scale=vc("sc", j))
                        self.tt("pool", m2, aa, aa, M)
                        self.ts("pool", m2, m2, -1.0, 1.0, M, AD)
                        yield
                        self.act(m2, m2, AF.Sqrt)
                        if first:
                            self.memset("pool", m2[:, 0:1], 1.0)
                        self.tt("dve", bi, xcs[jj], gx, M)
                        yield
                        self.tt("dve", bi, bi, m2, M)
                        so = self.cst_h + l * 8 + j
                        self.scan(hl, aa, bi, A[:, so:so + 1])
                        yield
                        self.copy("pool", A[:, so:so + 1], hl[:, T - 1:T])
                        self.tt("dve", oa3[:, j, :], hl, ybs[jj], M)
                        yield

                def interleave(ga_, gb_):
                    alive_a, alive_b = ga_ is not None, gb_ is not None
                    while alive_a or alive_b:
                        if alive_a:
                            try:
                                next(ga_)
                            except StopIteration:
                                alive_a = False
                        if alive_b:
                            try:
                                next(gb_)
                            except StopIteration:
                                alive_b = False
                interleave(gen_Lp(0), None)
                for hh in range(4):
                    interleave(gen_Lc(hh), gen_Lp(hh + 1) if hh < 3 else None)
                rs_ = rms_stats([oa3[:, j, :] for j in range(8)], DL, RMS_EPS)
                for j in range(8):
                    self.stt(mix3[:, j, :], oa3[:, j, :], vc("lrun", j), rs_, M, M)
                if ti == 0 and l == 0:
                    self.dump("oa0", A[:, oa_off:oa_off + 8 * T], [128, 8 * T])

                def gen_P1(j, l=l, ti=ti):
                    lo_off = self.lora_off[j % 2]
                    self.dma(A[:, lo_off:lo_off + 384], self.d_lora[l, j], "lora%d" % (j % 2))
                    lo = A[:, lo_off:lo_off + 384].rearrange("p (q m) -> p q m", m=128)
                    r_ = FV(rkv_off[0])
                    k_ = FV(rkv_off[1])
                    v_ = FV(rkv_off[2])
                    SL = 3
                    for which, dst in enumerate((r_, k_, v_)):
                        bank = self.p1big()
                        yield from proj_g(bank, 128, u3)
                        yield from slack(SL)
                        yield from shift_evac_g(bank, 128, which * 8 + j, dst)
                        yield
                    sgw, av, kap, kp, bvec, t1, t2 = [FV(full_off[n]) for n in
                                                      ("sgw", "av", "kap", "kp", "bvec", "t1", "t2")]
                    gb = self.bv(g_off[j % 2], T)
                    bvv = self.bv(bvb_off[j % 2], T)
                    t1b = self.bv(full_off["t1"], T)
                    bk = self.p1sm()
                    self.mm(bk[:, 0:T], lo[0:64, 0, :], tw[0:64, :])
                    yield from slack(SL)
                    self.act(sgw, bk[:, 0:T], AF.Sigmoid, bias=vc("w0", j))
                    yield
                    bk = self.p1sm()
                    self.mm(bk[:, 0:T], lo[64:128, 0, :], l40[64:128, :])
                    yield from slack(SL)
                    self.act(av, bk[:, 0:T], AF.Sigmoid, bias=vc("a0", j))
                    yield
                    if l == 0:
                        self.copy("pool", vf3[:, j, :], v_)
                    if l == 1:
                        bk = self.p1sm()
                        self.mm(bk[:, 0:T], lo[32:64, 2, :], l42[32:64, :])
                        yield from slack(SL)
                        self.act(t2, bk[:, 0:T], AF.Sigmoid, bias=vc("v0", j))
                        self.tt("pool", t1, vf3[:, j, :], v_, SUB)
                        yield from slack(SL)
                        self.tt("pool", t1, t1, t2, M)
                        self.tt("pool", v_, v_, t1, AD)
                        yield
                    bk = self.p1sm()
                    self.mm(bk[:, 0:T], lo[:, 1, :], sg1, start=True, stop=False)
                    self.mm(bk[:, 0:T], lo[0:32, 2, :], l42[0:32, :], start=False, stop=True)
                    yield from slack(SL)
                    self.act(gb, bk[:, 0:T], AF.Copy)
                    yield
                    self.act(t1b, k_, AF.Square, scale=vc("kkw", j))
                    self.ts("pool", kap, k_, vc("kkw", j), 0.0, M, AD)
                    self.ts("pool", t2, av, vc("ka", j), vc("omka", j), M, AD)
                    yield from slack(SL)
                    bk = self.p1sm()
                    self.mm(bk[:, 0:T], bonesbf, t1b)
                    self.tt("pool", kp, k_, t2, M)
                    self.ts("pool", t2, r_, vc("rk", j), 0.0, M, AD)
                    yield from slack(SL)
                    rn = FV(full_off["t1"])
                    self.act(rn, bk[:, 0:T], AF.Ln, bias=1e-18)
                    self.act(rn, rn, AF.Exp, scale=-0.5)
                    prb = self.bv(full_off["bvec"], T)
                    self.tt("pool", prb, t2, kp, M)
                    yield from slack(SL)
                    bk = self.p1sm()
                    self.mm(bk[:, 0:T], bonesbf, prb)
                    self.tt("pool", kap, kap, rn, M)
                    yield from slack(SL)
                    self.tt("pool", bvec, av, kap, M)
                    self.tt("dve", bvv, bk[:, 0:T], v_, M)
                    yield
                    if ti == 0 and l == 0 and j == 0:
                        self.dump("r0", r_, [128, T])
                        self.dump("kp0", kp, [128, T])
                        self.dump("v0", v_, [128, T])
                        self.dump("kap0", kap, [128, T])
                        self.dump("sgw0", sgw, [128, T])
                        self.dump("av0", av, [128, T])

                def gen_rest(j, part, l=l, ti=ti):
                    S2 = A[:, self.s2d_off + (l * 8 + j) * 128:self.s2d_off + (l * 8 + j) * 128 + 128]
                    r_ = FV(rkv_off[0])
                    v_ = FV(rkv_off[2])
                    sgw, kap, kp, bvec = [FV(full_off[n]) for n in ("sgw", "kap", "kp", "bvec")]
                    gb = self.bv(g_off[j % 2], T)
                    bvv = self.bv(bvb_off[j % 2], T)
                    NC8 = T // CH
                    Lc, Lex, Wt, Winv, Wprev, Wend, Ld = [FV(half_off[n]) for n in
                                                          ("Lc", "Lex", "Wt", "Winv", "Wprev", "Wend", "Ld")]
                    c3 = lambda ap: ap.rearrange("p (c t) -> p c t", t=CH)
                    wc = A[:, self.wc_off:self.wc_off + NC8]
                    bview = lambda off, x: self.bv(off, NC8 * x).rearrange("p (c x) -> p c x", x=x)
                    arm = bview(arm_off, 192)
                    bm, km, vm, khm, bhm = [bview(o, 128) for o in (bm_off, km_off, vm_off, khm_off, bhm_off)]
                    ams = bview(ams_off, 512)
                    bY = self.banks[7]
                    if part == "p2":
                        self.scan(Lc, cmask, sgw, 0.0)
                        yield
                        self.tt("pool", Lex, Lc, sgw, SUB)
                        self.tt("pool", c3(Ld), c3(Lc)[:, :, CH - 1:CH].broadcast_to([128, NC8, CH]), c3(Lc), SUB)
                        self.act(Winv, Lc, AF.Exp, scale=C0)
                        yield
                        self.act(Wt, Lc, AF.Exp, scale=-C0)
                        yield
                        self.act(Wprev, Lex, AF.Exp, scale=-C0)
                        self.act(Wend, Ld, AF.Exp, scale=-C0)
                        self.copy("pool", wc, c3(Wt)[:, :, CH - 1])
                        yield
                        pks = [FV(full_off["t2"]), FV(full_off["t1"])]
                        plan = ((kap, Wprev, arm, 2), (bvec, Winv, bm, 0), (kp, Winv, km, 0), (kp, Wend, khm, 0), (bvec, Wend, bhm, 0))
                        for qi, (xa, xw, dstv, mo) in enumerate(plan):
                            pk = pks[qi % 2]
                            self.tt("dve", pk, xa, xw, M)
                            yield
                            self.act(dstv[:, :, 0:64], c3(pk), AF.Identity, scale=hm[mo + 0])
                            self.ts("pool", dstv[:, :, 64:128], c3(pk), hm[mo + 1], 0.0, M, AD)
                            yield
                        for hp in range(2):
                            cs = slice(hp * 64, hp * 64 + 64)
                            self.ts("pool", vm[:, :, cs], c3(v_), hm[hp], 0.0, M, AD)
                        self.tt("dve", arm[:, :, 128:192], c3(r_), c3(Wt), M)
                        yield
                        return
                    if part == "o":
                        ys, yc, rs2 = FV(ys_off), FV(yc_off), FV(rs_off)
                        self.act(ys, bY[:, 0:T], AF.Copy)
                        if ti == 0 and l == 0 and j == 0:
                            self.dump("y0", ys[:, 0:HT], [128, HT])
                        yield
                        bk = self.rbank()
                        self.mm(bk[:, 0:T], bones, ys)
                        self.stt(yc, bk[:, 0:T], -1.0 / 64.0, ys, M, AD)
                        yield
                        ysb = self.bv(ys_off, T)
                        self.act(ysb, yc, AF.Square)
                        bk = self.rbank()
                        self.mm(bk[:, 0:T], bonesbf, ysb)
                        yield
                        self.act(rs2, bk[:, 0:T], AF.Ln, bias=float(LNX_EPS), scale=1.0 / 64.0)
                        self.act(rs2, rs2, AF.Exp, scale=-0.5)
                        yield
                        self.tt("dve", yc, yc, rs2, M)
                        self.ts("dve", yc, yc, vc("lnxw", j), vc("lnxb", j), M, AD)
                        yield
                        self.tt("pool", yc, yc, bvv, AD)
                        self.tt("dve", mix3[:, 8 + j, :], yc, gb, M)
                        yield
                        return
                    for c in range(NC8):
                        bk = self.rbank()
                        self.mm(bk[:, 0:128], arm[:, c, 0:128], bm[:, c, :])
                        self.mm(bk[:, 128:320], bm[:, c, :], arm[:, c, 0:192])
                        self.mm(bk[:, 320:512], km[:, c, :], arm[:, c, 0:192])
                        yield
                        self.tt("dve", ams[:, c, :], bk[:, :], maskall, M)
                        yield
                    vd, khd, bhd = [bview(o, 128) for o in (vd_off, khd_off, bhd_off)]
                    for src, off in ((vm, vd_off), (khm, khd_off), (bhm, bhd_off)):
                        bk = self.rbank()
                        bkb = bk[:, :].bitcast(BF16)
                        for c in range(NC8):
                            self.tr(bkb[:, c * 128:(c + 1) * 128], src[:, c, :], identbf)
                        self.act(self.bv(off, NC8 * 128), bkb, AF.Copy)
                        yield
                    acc = bview(acc_off, 128)
                    nbv, pbv = bview(nb_off, 128), bview(pbuf_off, 128)
                    identb3 = identbf.rearrange("p (o x) -> p o x", o=1).broadcast_to([128, NC8, 128])
                    self.tt("dve", acc, ams[:, :, 128:256], identb3, AD)
                    Ncur, Pcur = ams[:, :, 0:128], ams[:, :, 128:256]
                    for lev in range(1, 6):
                        bN = [self.rbank(), self.rbank()]
                        for c in range(NC8):
                            self.mm(bN[c // 4][:, (c % 4) * 128:(c % 4 + 1) * 128], Pcur[:, c, :], Ncur[:, c, :])
                        yield
                        if lev < 5:
                            bP = [self.rbank(), self.rbank()]
                            for c in range(NC8):
                                self.mm(bP[c // 4][:, (c % 4) * 128:(c % 4 + 1) * 128], Ncur[:, c, :], Pcur[:, c, :])
                        yield
                        for hb in range(2):
                            self.act(nbv[:, 4 * hb:4 * hb + 4, :], bN[hb][:, :].rearrange("p (c x) -> p c x", x=128), AF.Copy)
                        if lev < 5:
                            for hb in range(2):
                                self.copy("dve", pbv[:, 4 * hb:4 * hb + 4, :], bP[hb][:, :].rearrange("p (c x) -> p c x", x=128))
                            Pcur = pbv
                        Ncur = nbv
                        yield
                        yield
                        bA = [self.rbank(), self.rbank()]
                        for c in range(NC8):
                            self.mm(bA[c // 4][:, (c % 4) * 128:(c % 4 + 1) * 128], Ncur[:, c, :], acc[:, c, :])
                        for hb in range(2):
                            self.tt("dve", acc[:, 4 * hb:4 * hb + 4, :], bA[hb][:, :].rearrange("p (c x) -> p c x", x=128),
                                    acc[:, 4 * hb:4 * hb + 4, :], AD)
                        yield
                    bY = self.banks[7]
                    Xs = self.bv(xs_off, 128)
                    Us = self.bv(us_off, 128)
                    S2b = self.bv(s2b_off, 128)
                    self.copy("dve", S2b, S2)
                    for c in range(NC8):
                        bX = self.rbank()
                        self.mm(bX[:, 0:128], arm[:, c, 0:128], S2b, start=True, stop=False)
                        self.mm(bX[:, 0:128], ams[:, c, 320:448], vd[:, c, :], start=False, stop=True)
                        self.act(Xs, bX[:, 0:128], AF.Copy)
                        yield
                        self.mm(bX[:, 128:256], acc[:, c, :], Xs)
                        self.act(Us, bX[:, 128:256], AF.Copy)
                        yield
                        yo = bY[:, c * CH:(c + 1) * CH]
                        self.mm(yo, S2b, arm[:, c, 128:192], start=True, stop=False)
                        self.mm(yo, Us, ams[:, c, 256:320], start=False, stop=False)
                        self.mm(yo, vd[:, c, :], ams[:, c, 448:512], start=False, stop=True)
                        yield
                        self.mm(bX[:, 256:384], khd[:, c, :], vd[:, c, :], start=True, stop=False)
                        self.mm(bX[:, 256:384], bhd[:, c, :], Us, start=False, stop=True)
                        yield
                        self.stt(S2b, S2, wc[:, c:c + 1], bX[:, 256:384], M, AD)
                        self.stt(S2, S2, wc[:, c:c + 1], bX[:, 256:384], M, AD)
                        yield
                def drain(g):
                    for _ in g:
                        pass
                drain(gen_P1(0))
                drain(gen_rest(0, "p2"))
                for j in range(8):
                    interleave(gen_rest(j, "chain"), gen_P1(j + 1) if j < 7 else None)
                    interleave(gen_rest(j, "o"), gen_rest(j + 1, "p2") if j < 7 else None)
                if ti == 0 and l == 0:
                    self.dump("mix0", A[:, self.mix_off:self.mix_off + 16 * T // 2], [128, 16 * T // 2])

                for c in range(16):
                    bank = self.pbig()
                    proj(bank, 128, mix3)
                    self.tt("dve", h3[:, c, :], bank[:, 0:T], h3[:, c, :], AD)
                if ti == 0 and l == 0:
                    self.dump("hmix0", A[:, self.h_off:self.h_off + 16 * T], [128, 16 * T])

                rs_ = rms_stats([h3[:, c, :] for c in range(16)], D, RMS_EPS)
                for c in range(16):
                    self.stt(u3[:, c, :], h3[:, c, :], vc("ln_ffn", c), rs_, M, M)
                hid3 = self.bv(hid_off, 16 * T).rearrange("p (c t) -> p c t", t=T)
                for g in range(3):
                    for fi in range(16):
                        f = 16 * g + fi
                        bg = self.pbig()
                        proj(bg, 128, u3)
                        bu = self.pbig()
                        proj(bu, 128, u3)
                        gz = FV(gz_off[fi % 2], 2 + T)
                        ga = FV(gacc_off[fi % 2])
                        so = self.cst_ff + (l * 48 + f) * 2
                        self.copy("pool", gz[:, 0:2], A[:, so:so + 2])
                        self.act(gz[:, 2:2 + T], bg[:, 0:T], AF.Copy)
                        self.copy("pool", A[:, so:so + 2], gz[:, T:T + 2])
                        self.ts("dve", ga, gz[:, 2:2 + T], vc("cfw", 2 * 48 + f), vc("cfb", f), M, AD)
                        self.stt(ga, gz[:, 1:1 + T], vc("cfw", 48 + f), ga, M, AD)
                        self.stt(ga, gz[:, 0:T], vc("cfw", f), ga, M, AD)
                        self.act(ga, ga, AF.Gelu_apprx_tanh)
                        self.tt("dve", hid3[:, fi, :], bu[:, 0:T], ga, M)
                    for c in range(16):
                        bank = self.pbig()
                        proj(bank, 128, hid3)
                        self.tt("dve", h3[:, c, :], bank[:, 0:T], h3[:, c, :], AD)
                if ti == 0 and l == 0:
                    self.dump("hffn0", A[:, self.h_off:self.h_off + 16 * T], [128, 16 * T])

                rs_ = rms_stats([h3[:, c, :] for c in range(16)], D, RMS_EPS)
                for c in range(16):
                    self.stt(u3[:, c, :], h3[:, c, :], vc("ln_ple", c), rs_, M, M)
                pt = A[:, self.pt_off:self.pt_off + 2 * T].rearrange("p (k t) -> p k t", t=T)
                self.dma(pt, self.d_pT[l].rearrange("k p t -> p k t")[:, :, t0:t0 + T], "p")
                ptb = self.bv(ptb_off, 2 * T).rearrange("p (k t) -> p k t", t=T)
                self.copy("pool", self.bv(ptb_off, 2 * T), A[:, self.pt_off:self.pt_off + 2 * T])
                pt = ptb
                e3 = A[:, e_off:e_off + 16 * T].rearrange("p (c t) -> p c t", t=T)
                for c in range(16):
                    po = self.ppw_off[c % 2]
                    self.dma(A[:, po:po + 256], self.d_ppw[l, c], "ppw%d" % (c % 2))
                    self.copy("pool", self.bv(ppwb_off[c % 2], 256), A[:, po:po + 256])
                    pw = self.bv(ppwb_off[c % 2], 256).rearrange("p (k m) -> p k m", m=128)
                    bank = self.pbig()
                    proj(bank, 128, u3)
                    sg = FV(sgp_off[c % 2])
                    self.act(sg, bank[:, 0:T], AF.Sigmoid)
                    bk = self.psm()
                    for kc in range(2):
                        self.mm(bk[:, 0:T], pw[:, kc, :], pt[:, kc, :], start=(kc == 0), stop=(kc == 1))
                    self.tt("dve", e3[:, c, :], bk[:, 0:T], sg, M)
                rs_ = rms_stats([e3[:, c, :] for c in range(16)], D, RMS_EPS)
                for c in range(16):
                    self.tt("pool", e3[:, c, :], e3[:, c, :], rs_, M)
                    self.stt(h3[:, c, :], e3[:, c, :], vc("ln_plep", c), h3[:, c, :], M, AD)
                if ti == 0 and l == 0:
                    self.dump("hple0", A[:, self.h_off:self.h_off + 16 * T], [128, 16 * T])

            rs_ = rms_stats([h3[:, c, :] for c in range(16)], D, RMS_EPS)
            ob3 = A[:, ob_off:ob_off + 16 * T].rearrange("p (c t) -> p c t", t=T)
            for c in range(16):
                self.stt(ob3[:, c, :], h3[:, c, :], self.vcol(0, "ln_fin", c), rs_, M, M)
            self.finals.append(self.dma(self.d_out.rearrange("c p t -> p c t")[:, :, t0:t0 + T], ob3, "out"))

        self.P.emit(final_wait_ops=self.finals)
        return self.nc


_CACHE = {}


def kernel(**inputs):
    shared, percore = host_pack(inputs)
    if "nc" not in _CACHE:
        b = Builder()
        _CACHE["nc"] = b.build()
    nc = _CACHE["nc"]
    in_maps = []
    for c in range(NCORES):
        m = dict(shared)
        m.update(percore[c])
        in_maps.append(m)
    res = run_bass_kernel_spmd(nc, in_maps, core_ids=list(range(NCORES)))
    out = np.empty((NCORES, S, D), np.float32)
    for c in range(NCORES):
        o = np.asarray(res.results[c]["outT"]).reshape(D, S)
        out[c] = o.T
    return out
```
